# Optimizing a Trainium2 kernel written in Bass

```python
import jax, jax.numpy as jnp
from jax import lax
import numpy as np

D_MODEL = 1024
BATCH = 8
SEQ = 2048
DEPTH = 1

D_LRU = D_MODEL // 2
LRU_BLOCKS = 8
LRU_BW = D_LRU // LRU_BLOCKS
LRU_C = 8.0
CONV_WIDTH = 4
N_HEADS = 8
HEAD_DIM = 64
KV_GROUPS = 2
HEADS_PER_GROUP = N_HEADS // KV_GROUPS
D_ATT = N_HEADS * HEAD_DIM
D_KV = KV_GROUPS * HEAD_DIM
CMP_LEN = 32
CMP_STRIDE = 16
CMP_HID = 256
SEL_BLOCK = 64
SEL_TOPK = 8
WINDOW = 512
Q_BLOCK = 128
N_BRANCH = 3
FORCED_SCORE = 1.0e4
EPS = 1e-6
NEG = -1e30
SPLIT_SIZES = (D_LRU, D_LRU, D_ATT, D_KV, D_KV, D_KV, D_KV, D_KV, D_KV, D_ATT, N_HEADS * N_BRANCH)
D_IN = 2 * D_LRU + 2 * D_ATT + 6 * D_KV + N_HEADS * N_BRANCH
D_CAT = D_LRU + D_ATT

kernel_name = "hymba_rglru_nsa_alibi_layer"


def rms_norm(x, g):
    xf = x.astype(jnp.float32)
    y = xf * lax.rsqrt(jnp.mean(xf * xf, axis=-1, keepdims=True) + EPS)
    return (y * g.astype(jnp.float32)).astype(x.dtype)


def masked_softmax(s, mask):
    s = jnp.where(mask, s, NEG)
    m = jnp.max(s, axis=-1, keepdims=True)
    p = jnp.exp(s - m) * mask
    return p / jnp.maximum(jnp.sum(p, axis=-1, keepdims=True), 1e-30)


def alibi_slopes():
    return 2.0 ** (-8.0 * jnp.arange(1, N_HEADS + 1, dtype=jnp.float32) / N_HEADS)


def split_columns(proj):
    out, start = [], 0
    for size in SPLIT_SIZES:
        out.append(proj[..., start:start + size])
        start += size
    return out


def causal_dwconv(x, w, b):
    y = lax.conv_general_dilated(x, w[:, None, :].astype(x.dtype), window_strides=(1,),
                                 padding=[(CONV_WIDTH - 1, 0)],
                                 dimension_numbers=('NWC', 'WIO', 'NWC'),
                                 feature_group_count=x.shape[-1])
    return y + b


def rg_lru(x, w_a, b_a, w_i, b_i, lam):
    B, S, C = x.shape
    xb = x.reshape(B, S, LRU_BLOCKS, LRU_BW)
    r = jax.nn.sigmoid(jnp.einsum('bsnc,ncd->bsnd', xb, w_a).reshape(B, S, C) + b_a)
    i = jax.nn.sigmoid(jnp.einsum('bsnc,ncd->bsnd', xb, w_i).reshape(B, S, C) + b_i)
    log_a = -LRU_C * r.astype(jnp.float32) * jax.nn.softplus(-lam.astype(jnp.float32))
    a = jnp.exp(log_a)
    u = jnp.sqrt(-jnp.expm1(2.0 * log_a)) * (i * x).astype(jnp.float32)

    def combine(left, right):
        a_l, b_l = left
        a_r, b_r = right
        return a_l * a_r, a_r * b_l + b_r

    _, h = lax.associative_scan(combine, (a, u), axis=1)
    return h.astype(x.dtype)


def compress_blocks(k, pos, w1, w2):
    B, S, G, dh = k.shape
    chunks = k.reshape(B, S // CMP_STRIDE, CMP_STRIDE, G, dh)
    blocks = jnp.concatenate([chunks[:, :-1], chunks[:, 1:]], axis=2)
    blocks = blocks + pos[None, None, :, None, :]
    flat = blocks.transpose(0, 1, 3, 2, 4).reshape(B, blocks.shape[1], G, CMP_LEN * dh)
    hid = jax.nn.silu(jnp.einsum('bngf,fh->bngh', flat, w1))
    return jnp.einsum('bngh,hd->bngd', hid, w2)


def overlap_matrix(n_cmp, n_sel):
    cs = np.arange(n_cmp) * CMP_STRIDE
    ce = cs + CMP_LEN
    ss = np.arange(n_sel) * SEL_BLOCK
    se = ss + SEL_BLOCK
    ov = np.clip(np.minimum(ce[:, None], se[None, :]) - np.maximum(cs[:, None], ss[None, :]), 0, None)
    return jnp.asarray(ov / CMP_LEN, dtype=jnp.float32)


def nsa_mixer(q, k_cmp, v_cmp, k_slc, v_slc, k_win, v_win, gate_logits,
              kc_norm_g, cmp_pos, cmp_k_w1, cmp_k_w2, cmp_v_w1, cmp_v_w2):
    B, S = q.shape[0], q.shape[1]
    n_sel = S // SEL_BLOCK
    n_top = min(SEL_TOPK, n_sel)
    n_qb = S // Q_BLOCK
    scale = HEAD_DIM ** -0.5
    slopes = alibi_slopes().reshape(KV_GROUPS, HEADS_PER_GROUP)[None, None, :, :, None]
    t = jnp.arange(S)
    qg = q.reshape(B, S, KV_GROUPS, HEADS_PER_GROUP, HEAD_DIM)

    kc = rms_norm(compress_blocks(k_cmp, cmp_pos, cmp_k_w1, cmp_k_w2), kc_norm_g)
    vc = compress_blocks(v_cmp, cmp_pos, cmp_v_w1, cmp_v_w2)
    n_cmp = kc.shape[1]
    c_end = jnp.arange(n_cmp) * CMP_STRIDE + (CMP_LEN - 1)
    d_c = (t[:, None] - c_end[None, :])[None, :, None, None, :]
    s_c = (jnp.einsum('btgjd,bngd->btgjn', qg, kc).astype(jnp.float32) * scale
           - slopes * d_c.astype(jnp.float32))
    p_c = masked_softmax(s_c, d_c >= 0)
    o_c = jnp.einsum('btgjn,bngd->btgjd', p_c.astype(vc.dtype), vc)

    imp = jnp.einsum('btgjn,nm->btgm', p_c, overlap_matrix(n_cmp, n_sel))
    cur = (t // SEL_BLOCK)[:, None]
    blk = jnp.arange(n_sel)[None, :]
    forced = (blk == 0) | (blk == cur) | (blk == cur - 1)
    score = jnp.where(forced[None, :, None, :], FORCED_SCORE,
                      jnp.where((blk <= cur)[None, :, None, :], imp, -1.0))
    _, sel_idx = lax.top_k(score, n_top)

    k_blocks = k_slc.reshape(B, n_sel, SEL_BLOCK, KV_GROUPS, HEAD_DIM).transpose(0, 3, 1, 2, 4)
    v_blocks = v_slc.reshape(B, n_sel, SEL_BLOCK, KV_GROUPS, HEAD_DIM).transpose(0, 3, 1, 2, 4)
    pad = ((0, 0), (WINDOW, 0), (0, 0), (0, 0))
    k_win_p = jnp.pad(k_win, pad)
    v_win_p = jnp.pad(v_win, pad)
    gather = jax.vmap(jax.vmap(lambda blocks, idx: blocks[idx]))
    n_key = n_top * SEL_BLOCK

    def block_step(args):
        qb, ib, c = args
        tq = c * Q_BLOCK + jnp.arange(Q_BLOCK)
        flat = ib.transpose(0, 2, 1, 3).reshape(B, KV_GROUPS, Q_BLOCK * n_top)
        kg = gather(k_blocks, flat).reshape(B, KV_GROUPS, Q_BLOCK, n_key, HEAD_DIM)
        vg = gather(v_blocks, flat).reshape(B, KV_GROUPS, Q_BLOCK, n_key, HEAD_DIM)
        pos = (ib[..., None] * SEL_BLOCK + jnp.arange(SEL_BLOCK)).reshape(B, Q_BLOCK, KV_GROUPS, n_key)
        d_s = (tq[None, :, None, None] - pos)[:, :, :, None, :]
        s_s = (jnp.einsum('bqgjd,bgqkd->bqgjk', qb, kg).astype(jnp.float32) * scale
               - slopes * d_s.astype(jnp.float32))
        p_s = masked_softmax(s_s, d_s >= 0)
        o_s = jnp.einsum('bqgjk,bgqkd->bqgjd', p_s.astype(vg.dtype), vg)
        start = c * Q_BLOCK
        kw = lax.dynamic_slice_in_dim(k_win_p, start, WINDOW + Q_BLOCK, axis=1)
        vw = lax.dynamic_slice_in_dim(v_win_p, start, WINDOW + Q_BLOCK, axis=1)
        posw = start - WINDOW + jnp.arange(WINDOW + Q_BLOCK)
        d_w = tq[:, None] - posw[None, :]
        m_w = (d_w >= 0) & (d_w < WINDOW) & (posw >= 0)[None, :]
        s_w = (jnp.einsum('bqgjd,bkgd->bqgjk', qb, kw).astype(jnp.float32) * scale
               - slopes * d_w.astype(jnp.float32)[None, :, None, None, :])
        p_w = masked_softmax(s_w, m_w[None, :, None, None, :])
        o_w = jnp.einsum('bqgjk,bkgd->bqgjd', p_w.astype(vw.dtype), vw)
        return o_s, o_w

    q_blocks = qg.reshape(B, n_qb, Q_BLOCK, KV_GROUPS, HEADS_PER_GROUP, HEAD_DIM).transpose(1, 0, 2, 3, 4, 5)
    i_blocks = sel_idx.reshape(B, n_qb, Q_BLOCK, KV_GROUPS, n_top).transpose(1, 0, 2, 3, 4)
    o_s, o_w = lax.map(block_step, (q_blocks, i_blocks, jnp.arange(n_qb)))

    def unblock(o):
        return o.transpose(1, 0, 2, 3, 4, 5).reshape(B, S, KV_GROUPS, HEADS_PER_GROUP, HEAD_DIM)

    g = jax.nn.sigmoid(gate_logits).reshape(B, S, KV_GROUPS, HEADS_PER_GROUP, N_BRANCH)
    o = g[..., 0:1] * o_c + g[..., 1:2] * unblock(o_s) + g[..., 2:3] * unblock(o_w)
    return o.reshape(B, S, D_ATT)


def setup_inputs(seed: int = 0) -> dict:
    key = jax.random.key(seed)
    ks = jax.random.split(key, 24)

    def nrm(k, shape, scale):
        return jax.random.normal(k, shape, jnp.float32) * scale

    a0 = jax.random.uniform(ks[9], (DEPTH, D_LRU), jnp.float32, 0.9, 0.999)
    return {
        "x": nrm(ks[0], (BATCH, SEQ, D_MODEL), 1.0),
        "norm_g": 1.0 + nrm(ks[1], (DEPTH, D_MODEL), 0.02),
        "w_in": nrm(ks[2], (DEPTH, D_MODEL, D_IN), D_MODEL ** -0.5),
        "conv_w": nrm(ks[3], (DEPTH, CONV_WIDTH, D_LRU), CONV_WIDTH ** -0.5),
        "conv_b": nrm(ks[4], (DEPTH, D_LRU), 0.01),
        "lru_wa": nrm(ks[5], (DEPTH, LRU_BLOCKS, LRU_BW, LRU_BW), LRU_BW ** -0.5),
        "lru_ba": nrm(ks[6], (DEPTH, D_LRU), 0.01),
        "lru_wi": nrm(ks[7], (DEPTH, LRU_BLOCKS, LRU_BW, LRU_BW), LRU_BW ** -0.5),
        "lru_bi": nrm(ks[8], (DEPTH, D_LRU), 0.01),
        "lru_lambda": jnp.log(a0) - jnp.log1p(-a0),
        "q_norm_g": 1.0 + nrm(ks[10], (DEPTH, HEAD_DIM), 0.02),
        "kc_norm_g": 1.0 + nrm(ks[11], (DEPTH, HEAD_DIM), 0.02),
        "ks_norm_g": 1.0 + nrm(ks[12], (DEPTH, HEAD_DIM), 0.02),
        "kw_norm_g": 1.0 + nrm(ks[13], (DEPTH, HEAD_DIM), 0.02),
        "cmp_pos": nrm(ks[14], (DEPTH, CMP_LEN, HEAD_DIM), 0.1),
        "cmp_k_w1": nrm(ks[15], (DEPTH, CMP_LEN * HEAD_DIM, CMP_HID), (CMP_LEN * HEAD_DIM) ** -0.5),
        "cmp_k_w2": nrm(ks[16], (DEPTH, CMP_HID, HEAD_DIM), CMP_HID ** -0.5),
        "cmp_v_w1": nrm(ks[17], (DEPTH, CMP_LEN * HEAD_DIM, CMP_HID), (CMP_LEN * HEAD_DIM) ** -0.5),
        "cmp_v_w2": nrm(ks[18], (DEPTH, CMP_HID, HEAD_DIM), CMP_HID ** -0.5),
        "w_out": nrm(ks[19], (DEPTH, D_CAT, D_MODEL), D_CAT ** -0.5),
    }


def reference(x, norm_g, w_in, conv_w, conv_b, lru_wa, lru_ba, lru_wi, lru_bi, lru_lambda,
              q_norm_g, kc_norm_g, ks_norm_g, kw_norm_g, cmp_pos, cmp_k_w1, cmp_k_w2,
              cmp_v_w1, cmp_v_w2, w_out):
    B, S = x.shape[0], x.shape[1]
    for l in range(DEPTH):
        h = rms_norm(x, norm_g[l])
        proj = h @ w_in[l]
        (lru_x, lru_z, q, k_c, v_c, k_s, v_s, k_w, v_w, nsa_z, gate_logits) = split_columns(proj)
        lru = rg_lru(causal_dwconv(lru_x, conv_w[l], conv_b[l]),
                     lru_wa[l], lru_ba[l], lru_wi[l], lru_bi[l], lru_lambda[l])
        lru_out = lru * jax.nn.silu(lru_z)
        kv_shape = (B, S, KV_GROUPS, HEAD_DIM)
        q = rms_norm(q.reshape(B, S, N_HEADS, HEAD_DIM), q_norm_g[l])
        nsa = nsa_mixer(q, k_c.reshape(kv_shape), v_c.reshape(kv_shape),
                        rms_norm(k_s.reshape(kv_shape), ks_norm_g[l]), v_s.reshape(kv_shape),
                        rms_norm(k_w.reshape(kv_shape), kw_norm_g[l]), v_w.reshape(kv_shape),
                        gate_logits, kc_norm_g[l], cmp_pos[l], cmp_k_w1[l], cmp_k_w2[l],
                        cmp_v_w1[l], cmp_v_w2[l])
        nsa_out = nsa * jax.nn.silu(nsa_z)
        x = x + jnp.concatenate([lru_out, nsa_out], axis=-1) @ w_out[l]
    return x
```

```python
import numpy as np
import ml_dtypes
from contextlib import ExitStack
import concourse.bass as bass
import concourse.mybir as mybir
from concourse.bass_utils import run_bass_kernel_spmd

F32 = mybir.dt.float32
BF16 = mybir.dt.bfloat16
AF = mybir.ActivationFunctionType
ALU = mybir.AluOpType
S = 2048
D = 1024
DIN = 2840
NEGM = -240000.0
EPS = 1e-6
WARM = 0
NDUM = 0
PDUM = 0
DVE_RECIP = (2,)
DVE_RECIP_MINC = 2
BF = ml_dtypes.bfloat16


class Sched:
    def __init__(self, nc, es):
        self.nc = nc
        self.es = es
        self.E = {}
        for name, obj in (("pe", nc.tensor), ("dve", nc.vector), ("act", nc.scalar),
                          ("pool", nc.gpsimd), ("sp", nc.sync)):
            sem = es.enter_context(nc.semaphore("s_" + name))
            self.E[name] = {"obj": obj, "sem": sem, "cnt": 0, "waited": {}, "name": name}
        self.lastw = {}
        self.readers = {}
        self.dsems = {}

    def _deps(self, E, reads, writes):
        toks = []
        me = E["name"]
        for r in reads:
            t = self.lastw.get(r)
            if t is not None and not (t[2] == "pe" and me == "pe"):
                toks.append(t)
            if is_psum_key(r):
                for t in self.readers.get(r, ()):
                    if t[2] != me:
                        toks.append(t)
        for w in writes:
            t = self.lastw.get(w)
            if t is not None and t[2] != me:
                toks.append(t)
            for t in self.readers.get(w, ()):
                if t[2] != me:
                    toks.append(t)
        need = {}
        for t in toks:
            k = id(t[0])
            if k not in need or need[k][1] < t[1]:
                need[k] = (t[0], t[1])
        for k, (sem, val) in need.items():
            if E["waited"].get(k, 0) < val:
                E["obj"].wait_ge(sem, val)
                E["waited"][k] = val

    def _commit(self, tok, reads, writes):
        for w in writes:
            self.lastw[w] = tok
            self.readers[w] = []
        for r in reads:
            self.readers.setdefault(r, []).append(tok)

    def op(self, eng, fn, reads=(), writes=()):
        E = self.E[eng]
        self._deps(E, reads, writes)
        ins = fn(E["obj"])
        E["cnt"] += 1
        ins.then_inc(E["sem"], 1)
        self._commit([E["sem"], E["cnt"], eng], reads, writes)

    def dma(self, q, out, in_, reads, writes, dkey):
        E = self.E[q]
        self._deps(E, reads, writes)
        if dkey not in self.dsems:
            self.dsems[dkey] = [self.es.enter_context(self.nc.semaphore("d_" + dkey)), 0, []]
        ds = self.dsems[dkey]
        ins = E["obj"].dma_start(out=out, in_=in_)
        ds[1] += 16
        ins.then_inc(ds[0], 16)
        tok = [ds[0], ds[1], "dma"]
        ds[2].append(tok)
        self._commit(tok, reads, writes)

    def finalize_group(self, dkey):
        ds = self.dsems[dkey]
        for t in ds[2]:
            t[1] = ds[1]

    def fence(self, keys):
        toks = [[E["sem"], E["cnt"], "fence_" + n] for n, E in self.E.items() if E["cnt"] > 0]
        for k in keys:
            self.readers[k] = list(toks) + list(self.readers.get(k, []))

    def wait_all_dma(self, q, dkeys):
        E = self.E[q]
        for dk in dkeys:
            ds = self.dsems[dk]
            E["obj"].wait_ge(ds[0], ds[1])


def is_psum_key(k):
    if isinstance(k, tuple):
        return k[0] in ("ps", "ors")
    return k in ("psG", "psI", "psT")


def slopes():
    return [2.0 ** (-(h + 1)) for h in range(8)]


def host_consts():
    c = {}
    ident = np.eye(128, dtype=np.float32)
    blockones = np.kron(np.eye(2), np.ones((64, 64))).astype(np.float32)
    ones = np.ones((128, 128), np.float32)
    i = np.arange(128)[:, None]
    j = np.arange(128)[None, :]
    tri = np.where(j >= i, 0.0, NEGM).astype(np.float32)
    atri = np.where(j < i, 0.0, NEGM).astype(np.float32)
    c["cb"] = np.concatenate([ident, blockones, ones, tri, atri], axis=1).astype(BF)
    n = np.arange(128)[:, None]
    t = np.arange(S)[None, :]
    c["mc"] = np.where(t >= 16 * n + 31, 0.0, NEGM).astype(BF)
    selh = np.zeros((128, 24, 128), np.float32)
    for k in range(24):
        selh[k, k, :] = 1.0
        selh[32 + k, k, :] = 1.0
    c["selh"] = selh.reshape(128, 24 * 128).astype(BF)
    cs = np.arange(127) * 16
    ce = cs + 32
    ss = np.arange(32) * 64
    se = ss + 64
    ov = np.clip(np.minimum(ce[:, None], se[None, :]) - np.maximum(cs[:, None], ss[None, :]), 0, None) / 32.0
    ova = np.zeros((128, 33), np.float32)
    ova[:127, :32] = ov
    ova[:127, 32] = 1.0
    c["ova"] = ova.astype(BF)
    k = np.arange(S)
    krs = np.zeros((37, S), np.float32)
    krs[:32] = (k[None, :] // 64 == np.arange(32)[:, None])
    krs[32] = 1.0
    krs[33] = 1.0
    krs[34] = k % 128
    krs[35] = 1.0
    krs[36] = k // 128
    krw = krs.copy()
    krw[:32] = 0.0
    krc = np.zeros((37, 254), np.float32)
    krc[32] = 1.0
    krc[33] = 1.0
    krc[34] = 16.0 * (np.arange(254) % 127)
    krc[35] = 1.0
    krc[36] = 31.0 / 128.0
    c["krs"] = krs.astype(BF)
    c["krw"] = krw.astype(BF)
    c["krc"] = krc.astype(BF)
    jj = k % 512
    qrows = np.zeros((8, 5, S), np.float32)
    for h, sl in enumerate(slopes()):
        qrows[h, 0] = -8.0 * sl * (jj % 256)
        qrows[h, 1] = -8.0 * sl * 256.0 * (jj // 256)
        qrows[h, 2] = 8.0 * sl
        qrows[h, 3] = -8.0 * sl * 512.0 * (k // 512)
        qrows[h, 4] = 8.0 * sl * 128.0
    c["qrows"] = qrows.astype(BF)
    bsw = np.zeros((128, 128), np.float32)
    bc = np.zeros((128, 32), np.float32)
    for h, sl in enumerate(slopes()):
        for dl in range(-3, 13):
            bsw[:, h * 16 + dl + 3] = -sl * 128.0 * dl
        for cc in range(4):
            bc[:, h * 4 + cc] = sl * (16.0 * np.arange(128) + 31.0 - 512.0 * cc)
    A = np.zeros((128, 16, 32), np.float32)
    Bm = np.zeros((128, 16, 32), np.float32)
    for tt in range(16):
        tpos = tt * 128 + np.arange(128)
        cur = (tpos // 64)[:, None]
        m = np.arange(32)[None, :]
        forced = (m == 0) | (m == cur) | (m == cur - 1)
        A[:, tt, :] = np.where(forced, 0.0, np.where(m <= cur, 1.0, 0.0))
        Bm[:, tt, :] = np.where(forced, 1.0e4, np.where(m <= cur, 0.0, -1.0))
    cv = np.zeros((128, 3), np.float32)
    cv[:, 0] = EPS
    cv[:, 1] = 1.0
    cv[:, 2] = 1e-30
    c["cf"] = np.concatenate([bsw, bc, A.reshape(128, 512), Bm.reshape(128, 512), cv], axis=1).astype(np.float32)
    return c


CF_BSW, CF_BC, CF_A, CF_B, CF_CV = 0, 128, 160, 672, 1184
CF_W = 1187
V_G, V_CW, V_CB, V_BA, V_BI, V_LAM, V_GQ, V_GKS, V_GKW, V_GKC = 0, 8, 24, 28, 32, 36, 40, 41, 42, 43
V_W = 44


def build(debug=(), stop_after=None, branches=(0, 1, 2)):
    nc = bass.Bass("TRN2", target_bir_lowering=False)
    es = ExitStack()
    with es:
        sc = Sched(nc, es)

        def din(name, shape, dt=F32):
            return nc.dram_tensor(name, list(shape), dt, kind="ExternalInput").ap()

        xT = din("xT", [D, S])
        w_in = din("w_in", [D, DIN])
        w_out = din("w_out", [D, D])
        vec = din("vec", [128, V_W])
        wa_d = din("wa", [128, 512])
        wi_d = din("wi", [128, 512])
        pos_d = din("posT2", [128, 16])
        w1k_d = din("w1k", [128, 4096])
        w1v_d = din("w1v", [128, 4096])
        w2k_d = din("w2k", [128, 128])
        w2v_d = din("w2v", [128, 128])
        cb_d = din("cb", [128, 640], BF16)
        mc_d = din("mc", [128, S], BF16)
        selh_d = din("selh", [128, 3072], BF16)
        ova_d = din("ova", [128, 33], BF16)
        krs_d = din("krs", [37, S], BF16)
        krw_d = din("krw", [37, S], BF16)
        krc_d = din("krc", [37, 254], BF16)
        qrows_d = din("qrows", [8, 5, S], BF16)
        cf_d = din("cf", [128, CF_W])
        yT = nc.dram_tensor("yT", [D, S], F32, kind="ExternalOutput").ap()

        def sb(name, shape, dt):
            return es.enter_context(nc.sbuf_tensor(name, list(shape), dt))

        def ps(name):
            return es.enter_context(nc.psum_tensor(name, [128, 512], F32))

        A1 = sb("A1", [128, 12288], F32)
        A2 = sb("A2", [128, 16384], BF16)
        A3 = sb("A3", [128, 6144], F32)
        Kc = sb("Kc", [128, 254], BF16)
        vctok = sb("vctok", [128, 256], BF16)
        vtok = sb("vtok", [128, 2 * 16 * 256], BF16)
        siluz = sb("siluz", [128, 4 * S], BF16)
        SIG = sb("SIG", [128, S], BF16)
        catT = sb("catT", [128, 8 * S], BF16)
        hidT = sb("hidT", [128, 2 * 508], BF16)
        T0 = sb("T0", [128, 1024], F32)
        T1 = sb("T1", [128, 1024], BF16)
        T2 = sb("T2", [128, 1024], F32)
        cb = sb("cbs", [128, 640], BF16)
        mc = sb("mcs", [128, S], BF16)
        selh = sb("selhs", [128, 3072], BF16)
        ova = sb("ovas", [128, 33], BF16)
        cf = sb("cfs", [128, CF_W], F32)
        vt = sb("vecs", [128, V_W], F32)
        wab = sb("wab", [128, 512], BF16)
        wib = sb("wib", [128, 512], BF16)
        posb = sb("posb", [128, 16], BF16)
        w2kb = sb("w2kb", [128, 128], BF16)
        w2vb = sb("w2vb", [128, 128], BF16)
        small = sb("small", [128, 64], F32)
        impacc = sb("impacc", [128, 128], F32)
        impt = sb("impt", [128, 128], F32)
        score = sb("score", [128, 128], F32)
        selt = sb("selt", [128, 256], BF16)
        VT = T0[:, :].bitcast(BF16)
        PS = [ps("ps%d" % i) for i in range(8)]

        ident = cb[:, 0:128]
        blockones = cb[:, 128:256]
        ones = cb[:, 256:384]
        TRI = cb[:, 384:512]
        ATRI = cb[:, 512:640]
        eps_ap = cf[:, CF_CV:CF_CV + 1]
        one_ap = cf[:, CF_CV + 1:CF_CV + 2]

        B = [A1[:, i * 2048:(i + 1) * 2048] for i in range(5)]
        XCb = [A1[:, 10240 + i * 1024:10240 + (i + 1) * 1024].bitcast(BF16) for i in range(2)]
        Qh = [A1[:, h * 1024:(h + 1) * 1024].bitcast(BF16) for h in range(8)]
        Kt = [A1[:, 8192 + i * 1024:8192 + (i + 1) * 1024].bitcast(BF16) for i in range(4)]
        xb = [A2[:, kc * 2048:(kc + 1) * 2048] for kc in range(8)]
        rstd = A3[:, 0:2048]
        wst = [A3[:, 2048 + b * 2048:2048 + (b + 1) * 2048].bitcast(BF16) for b in range(2)]
        KCB = [siluz[:, 0:2048], siluz[:, 2048:4096]]
        KB2 = [siluz[:, 4096:6144], siluz[:, 6144:8192]]
        w1kb = catT[:, 4 * S:6 * S]
        w1vb = catT[:, 6 * S:8 * S]

        def cload(q, dst, src, key):
            sc.dma(q, dst, src, (), (key,), "c_" + q)

        cload("sp", cb[:], cb_d, "cb")
        cload("sp", cf[:], cf_d, "cf")
        cload("sp", vt[:], vec, "vec")
        cload("sp", mc[:], mc_d, "mc")
        cload("sp", selh[:], selh_d, "selh")
        cload("sp", ova[:], ova_d, "ova")
        cload("pool", wab[:], wa_d, "wab")
        cload("pool", wib[:], wi_d, "wib")
        cload("pool", posb[:], pos_d, "posb")
        cload("pool", w2kb[:], w2k_d, "w2kb")
        cload("pool", w2vb[:], w2v_d, "w2vb")
        cload("pool", w1kb, w1k_d, "w1kb")
        cload("pool", w1vb, w1v_d, "w1vb")
        sc.finalize_group("c_sp")
        sc.finalize_group("c_pool")
        CONST = ("cb", "cf", "vec")

        def tgs(tg):
            return slice(tg * 512, (tg + 1) * 512)

        def rfast(e, out, in_):
            return e.reciprocal_approx_fast(out, in_)

        xTv = xT.rearrange("(kc p) t -> kc p t", p=128)
        Xs = [A1[:, kc * 2048:(kc + 1) * 2048] for kc in range(6)] + [A3[:, 2048:4096], A3[:, 4096:6144]]

        def xkeys(kc):
            if kc < 5:
                return tuple(("B", kc, tg) for tg in range(4))
            if kc == 5:
                return ("X5",)
            return (("wst", kc - 6),)

        for kc in range(8):
            sc.dma("sp", Xs[kc], xTv[kc], (), xkeys(kc), "xs%d" % kc)
        for kc in range(8):
            for tg in range(4):
                j = (kc * 4 + tg) % 2
                t1 = T1[:, j * 512:(j + 1) * 512]
                sc.op("act", lambda e: e.activation(t1, Xs[kc][:, tgs(tg)], AF.Square),
                      reads=xkeys(kc), writes=(("T1", j),))
                sc.op("pe", lambda e: e.matmul(PS[tg][:, :], ones, t1, start=(kc == 0), stop=(kc == 7)),
                      reads=(("T1", j), "cb"), writes=(("ps", tg),))
        for tg in range(4):
            sc.op("act", lambda e: e.activation(rstd[:, tgs(tg)], PS[tg][:, :], AF.Ln, bias=eps_ap, scale=1.0 / D),
                  reads=(("ps", tg), "cf"), writes=(("rstd", tg),))
            sc.op("act", lambda e: e.activation(rstd[:, tgs(tg)], rstd[:, tgs(tg)], AF.Exp, scale=-0.5),
                  reads=(("rstd", tg),), writes=(("rstd", tg),))
        for kc in (6, 7, 0, 1, 2, 3, 4, 5):
            for tg in range(4):
                sc.op("dve", lambda e: e.scalar_tensor_tensor(xb[kc][:, tgs(tg)], Xs[kc][:, tgs(tg)], vt[:, V_G + kc:V_G + kc + 1],
                                                              rstd[:, tgs(tg)], ALU.mult, ALU.mult),
                      reads=xkeys(kc) + (("rstd", tg), "vec"), writes=(("xb", kc),))

        dbg = {}
        if stop_after == "p1":
            dbg["rstd"] = (rstd, ("rstd", 3), [128, S], F32)
            dbg["xb0"] = (xb[0], ("xb", 0), [128, S], BF16)
            dbg["xb7"] = (xb[7], ("xb", 7), [128, S], BF16)
            return finish(nc, sc, es, yT, dbg, catT, debug)

        w_inv = w_in.rearrange("(kc p) c -> p kc c", p=128)
        groups = [(0, 512), (512, 512), (1024, 512), (1536, 512), (2048, 512), (2560, 280)]

        GBUF = {0: 0, 1: 1, 2: 0, 3: 2, 4: 1, 5: 2}
        wst.append(A3[:, 0:2048].bitcast(BF16))
        sc.fence([("wst", 2)])

        def load_group(gi):
            c0, n = groups[gi]
            b = GBUF[gi]
            dst = wst[b].rearrange("p (kc c) -> p kc c", kc=8)[:, :, 0:n]
            sc.dma("pool", dst, w_inv[:, :, c0:c0 + n], (), (("wst", b),), "wst%d" % b)

        load_group(0)
        load_group(1)
        load_group(3)
        sc.op("pool", lambda e: e.memset(SIG[:, :], 0.0), writes=tuple(("SIG", tg) for tg in range(4)))
        sc.op("pool", lambda e: e.memset(vtok[:, :], 1.0), writes=tuple(("vtok", w_, tg) for w_ in range(2) for tg in range(4)))
        sc.op("pool", lambda e: e.memset(vctok[:, :], 1.0), writes=("vctok",))
        sc.dma("sp", Kc[64:101, :], krc_d, (), ("Kcr",), "kcr")

        sc.op("act", lambda e: e.activation(small[:, 0:4], vt[:, V_LAM:V_LAM + 4], AF.Exp, scale=-1.0),
              reads=("vec",), writes=("cc",))
        sc.op("act", lambda e: e.activation(small[:, 0:4], small[:, 0:4], AF.Ln, bias=one_ap, scale=1.0),
              reads=("cc", "cf"), writes=("cc",))
        sc.op("dve", lambda e: e.tensor_scalar(small[:, 0:4], small[:, 0:4], -8.0, None, ALU.mult),
              reads=("cc",), writes=("cc",))

        XC = A1[:, 10240:11264].bitcast(BF16)
        sc.fence([("XCb", tg) for tg in range(4)])

        jobs = []

        def add_job(gi, off, ncols, tg, post, after=None, staged=False):
            jobs.append(dict(gi=gi, off=off, ncols=ncols, tg=tg, post=post, after=after, staged=staged))

        def post_x(c):
            def f(tg, P, pk):
                cw = lambda k: vt[:, V_CW + c * 4 + k:V_CW + c * 4 + k + 1]
                lo, hi = tg * 512, (tg + 1) * 512

                def st0():
                    sc.op("dve", lambda e: e.tensor_copy(B[0][:, lo:hi], P[:, :]), reads=(pk,), writes=(("B", 0, tg),))
                    sc.op("dve", lambda e: e.tensor_scalar(B[1][:, lo:hi], B[0][:, lo:hi], cw(3), vt[:, V_CB + c:V_CB + c + 1], ALU.mult, ALU.add),
                          reads=(("B", 0, tg), "vec"), writes=(("B", 1, tg),))
                    prev = ((("B", 0, tg - 1),) if tg > 0 else ())
                    for sh, k in ((1, 2), (2, 1), (3, 0)):
                        l2 = max(lo, sh)
                        sc.op("dve", lambda e, sh=sh, k=k, l2=l2: e.scalar_tensor_tensor(
                            B[1][:, l2:hi], B[0][:, l2 - sh:hi - sh], cw(k), B[1][:, l2:hi], ALU.mult, ALU.add),
                            reads=(("B", 0, tg), ("B", 1, tg), "vec") + prev, writes=(("B", 1, tg),))

                def st1():
                    sc.op("act", lambda e: e.activation(XC[:, lo:hi], B[1][:, lo:hi], AF.Copy), reads=(("B", 1, tg),), writes=(("XCb", tg),))
                    for wt, pidx in ((wab, 4), (wib, 5)):
                        sc.op("pe", lambda e, wt=wt, pidx=pidx: e.matmul(PS[pidx][:, :], wt[:, c * 128:(c + 1) * 128], XC[:, lo:hi],
                                                                         start=True, stop=True),
                              reads=(("XCb", tg), "wab", "wib"), writes=(("ps", pidx),))
                    sc.op("act", lambda e: e.activation(B[2][:, lo:hi], PS[4][:, :], AF.Sigmoid, bias=vt[:, V_BA + c:V_BA + c + 1], scale=1.0),
                          reads=(("ps", 4), "vec"), writes=(("B", 2, tg),))
                    sc.op("act", lambda e: e.activation(B[3][:, lo:hi], PS[5][:, :], AF.Sigmoid, bias=vt[:, V_BI + c:V_BI + c + 1], scale=1.0),
                          reads=(("ps", 5), "vec"), writes=(("B", 3, tg),))
                    sc.op("pool", lambda e: e.tensor_tensor(B[3][:, lo:hi], B[3][:, lo:hi], B[1][:, lo:hi], ALU.mult),
                          reads=(("B", 3, tg), ("B", 1, tg)), writes=(("B", 3, tg),))

                def st2():
                    sc.op("act", lambda e: e.activation(B[2][:, lo:hi], B[2][:, lo:hi], AF.Exp, scale=small[:, c:c + 1]),
                          reads=(("B", 2, tg), "cc"), writes=(("B", 2, tg),))
                    sc.op("dve", lambda e: e.tensor_tensor(B[4][:, lo:hi], B[2][:, lo:hi], B[2][:, lo:hi], ALU.mult),
                          reads=(("B", 2, tg),), writes=(("B", 4, tg),))

                def st3():
                    sc.op("act", lambda e: e.activation(B[4][:, lo:hi], B[4][:, lo:hi], AF.Ln, bias=one_ap, scale=-1.0),
                          reads=(("B", 4, tg), "cf"), writes=(("B", 4, tg),))
                    sc.op("act", lambda e: e.activation(B[4][:, lo:hi], B[4][:, lo:hi], AF.Exp, scale=0.5),
                          reads=(("B", 4, tg),), writes=(("B", 4, tg),))
                    sc.op("pool", lambda e: e.tensor_tensor(B[3][:, lo:hi], B[3][:, lo:hi], B[4][:, lo:hi], ALU.mult),
                          reads=(("B", 3, tg), ("B", 4, tg)), writes=(("B", 3, tg),))

                def st4():
                    if tg > 0:
                        sc.op("dve", lambda e: e.scalar_tensor_tensor(B[3][:, lo:lo + 1], B[2][:, lo:lo + 1], B[4][:, lo - 1:lo],
                                                                      B[3][:, lo:lo + 1], ALU.mult, ALU.add),
                              reads=(("B", 2, tg), ("B", 3, tg), ("B", 4, tg - 1)), writes=(("B", 3, tg),))
                    sc.op("dve", lambda e: e.tensor_tensor_scan(B[4][:, lo:hi], B[2][:, lo:hi], B[3][:, lo:hi], 0.0, ALU.mult, ALU.add),
                          reads=(("B", 2, tg), ("B", 3, tg)), writes=(("B", 4, tg),))
                return [st0, None, st1, None, st2, None, st3, None, st4]
            return f

        def post_z(c):
            def f(tg, P, pk):
                j = tg % 2
                lo, hi = tg * 512, (tg + 1) * 512
                zt = T0[:, j * 512:(j + 1) * 512] if tg < 2 else A1[:, 11264 + j * 512:11264 + (j + 1) * 512]
                zk = ("T0", j) if tg < 2 else ("ZS", j)

                def st0():
                    sc.op("act", lambda e: e.activation(zt, P[:, :], AF.Sigmoid), reads=(pk,), writes=(zk,))
                    sc.op("dve", lambda e: e.tensor_tensor(zt, zt, P[:, :], ALU.mult), reads=(pk, zk), writes=(zk,))

                def st2():
                    sc.op("pool", lambda e: e.tensor_tensor(catT[:, c * S + lo:c * S + hi], B[4][:, lo:hi], zt, ALU.mult),
                          reads=(("B", 4, tg), zk), writes=(("cat", c),))
                return [st0, None, None, None, None, None, st2]
            return f

        def setup_qk():
            qkeys = ([("Qd", h, tg) for h in range(8) for tg in range(4)] + [("Qs", h, c) for h in range(8) for c in range(4)]
                     + [("Qr", h) for h in range(8)] + [("Kd", i, tg) for i in range(4) for tg in range(4)]
                     + [("Kr", i) for i in range(4)])
            sc.fence(qkeys)
            for h in range(8):
                sc.op("pool", lambda e, h=h: e.memset(Qh[h][64:96, :], 0.0), writes=tuple(("Qs", h, c) for c in range(4)))
                sc.dma("sp", Qh[h][96:101, :], qrows_d[h], (), (("Qr", h),), "qr")
            for i in range(4):
                sc.dma("sp", Kt[i][64:101, :], krs_d if i < 2 else krw_d, (), (("Kr", i),), "qr")
            sc.finalize_group("qr")

        def post_norm(gcol, outs_fn):
            def f(tg, P, pk):
                j = tg % 2
                fs = slice(j * 512, (j + 1) * 512)
                t1, t2 = T1[:, fs], T2[:, fs]

                def st0():
                    sc.op("act", lambda e: e.activation(t1, P[:, :], AF.Square), reads=(pk,), writes=(("T1", j),))
                    sc.op("pe", lambda e: e.matmul(PS[4 + j][:, :], blockones, t1, start=True, stop=True),
                          reads=(("T1", j), "cb"), writes=(("ps", 4 + j),))

                def st1():
                    sc.op("act", lambda e: e.activation(t2, PS[4 + j][:, :], AF.Ln, bias=eps_ap, scale=1.0 / 64),
                          reads=(("ps", 4 + j), "cf"), writes=(("T2", j),))
                    sc.op("act", lambda e: e.activation(t2, t2, AF.Exp, scale=-0.5), reads=(("T2", j),), writes=(("T2", j),))
                    for half, (dst, key) in enumerate(outs_fn(tg)):
                        rows = slice(half * 64, (half + 1) * 64)
                        sc.op("dve", lambda e, dst=dst, rows=rows: e.scalar_tensor_tensor(
                            dst, P[rows, :], vt[rows, gcol:gcol + 1], t2[rows, :], ALU.mult, ALU.mult),
                            reads=(pk, ("T2", j), "vec"), writes=(key,))
                return [st0, st1]
            return f

        PS6b = PS[6][:, 0:256].bitcast(BF16)

        def post_v(which):
            def f(tg, P, pk):
                jv = tg % 2
                vts = T1[:, jv * 512:(jv + 1) * 512]
                sc.op("dve", lambda e: e.tensor_copy(vts, P[:, :]), reads=(pk,), writes=(("T1", jv),))
                for q4 in range(4):
                    sc.op("pe", lambda e, q4=q4: e.transpose(PS6b[:, q4 * 128:(q4 + 1) * 128], vts[:, q4 * 128:(q4 + 1) * 128], ident),
                          reads=(("T1", jv), "cb"), writes=(("ps", 6),))
                base = which * 4096 + tg * 1024
                dst = vtok[:, base:base + 1024].rearrange("p (k g x) -> p k g x", k=4, g=2)[:, :, :, 0:64]
                src = PS6b[:, 0:512].rearrange("p (k g d) -> p k g d", k=4, g=2)
                sc.op("act", lambda e: e.activation(dst, src, AF.Copy), reads=(("ps", 6),), writes=(("vtok", which, tg),))
            return f

        def k_outs(i0):
            return lambda tg: [(Kt[i0][0:64, tgs(tg)], ("Kd", i0, tg)), (Kt[i0 + 1][0:64, tgs(tg)], ("Kd", i0 + 1, tg))]


        def compression():
            w1v_ = [w1kb.rearrange("p (c h) -> p c h", c=16), w1vb.rearrange("p (c h) -> p c h", c=16)]
            w2v_ = [w2kb.rearrange("p (a d) -> p a d", a=2), w2vb.rearrange("p (a d) -> p a d", a=2)]
            for kv in range(2):
                for half in range(2):
                    col = 400 + kv * 2 + half
                    for c in range(16):
                        sc.op("pe", lambda e, c=c, col=col, kv=kv, half=half: e.matmul(
                            PS[7][:, col:col + 1], w1v_[kv][:, c, half * 128:(half + 1) * 128], posb[:, c:c + 1],
                            start=(c == 0), stop=(c == 15)),
                            reads=("w1kb", "w1vb", "posb"), writes=(("ps", 7),))
            sc.op("dve", lambda e: e.tensor_copy(small[:, 4:8], PS[7][:, 400:404]), reads=(("ps", 7),), writes=("bpos",))
            yield
            for kv in range(2):
                for g in range(2):
                    rows = slice(g * 64, (g + 1) * 64)
                    src = KCB[kv][rows, :].rearrange("p (n s) -> p s n", s=16)
                    xc = KB2[g][:, 0:2032].rearrange("p (c n) -> p c n", c=16)
                    for lpar in range(2):
                        prow = slice(lpar * 64, (lpar + 1) * 64)
                        for hi_, eng in ((0, "dve"), (1, "dve")):
                            sc.op(eng, lambda e, lpar=lpar, prow=prow, hi_=hi_: e.tensor_copy(
                                xc[prow, hi_ * 8:(hi_ + 1) * 8, :], src[:, lpar::2, hi_:hi_ + 127]),
                                reads=(("KCB", kv),), writes=(("KB2", g),))
                    yield
                for g in range(2):
                    for half in range(2):
                        o0 = (half * 2 + g) * 127
                        for c in range(16):
                            sc.op("pe", lambda e, c=c, g=g, o0=o0, half=half: e.matmul(
                                PS[6][:, o0:o0 + 127], w1v_[kv][:, c, half * 128:(half + 1) * 128],
                                KB2[g][:, c * 127:(c + 1) * 127], start=(c == 0), stop=(c == 15)),
                                reads=(("KB2", g), "w1kb", "w1vb"), writes=(("ps", 6),))
                for half in range(2):
                    sc.op("act", lambda e, half=half: e.activation(
                        hidT[:, kv * 508 + half * 254:kv * 508 + (half + 1) * 254], PS[6][:, half * 254:(half + 1) * 254],
                        AF.Silu, bias=small[:, 4 + kv * 2 + half:5 + kv * 2 + half], scale=1.0),
                        reads=(("ps", 6), "bpos"), writes=(("hidT", kv),))
                if kv == 0:
                    for g in range(2):
                        for half in range(2):
                            sc.op("pe", lambda e, g=g, half=half: e.matmul(
                                PS[7][0:64, g * 127:(g + 1) * 127], w2v_[0][:, half, :],
                                hidT[:, half * 254 + g * 127:half * 254 + (g + 1) * 127], start=(half == 0), stop=(half == 1)),
                                reads=(("hidT", 0), "w2kb"), writes=(("ps", 7),))
                    sc.op("act", lambda e: e.activation(T1[0:64, 0:254], PS[7][0:64, 0:254], AF.Square),
                          reads=(("ps", 7),), writes=(("T1", 0),))
                    sc.op("pe", lambda e: e.matmul(PS[4][0:64, 0:254], blockones[0:64, 0:64], T1[0:64, 0:254], start=True, stop=True),
                          reads=(("T1", 0), "cb"), writes=(("ps", 4),))
                    sc.op("act", lambda e: e.activation(T2[0:64, 0:254], PS[4][0:64, 0:254], AF.Ln, bias=eps_ap[0:64, :], scale=1.0 / 64),
                          reads=(("ps", 4), "cf"), writes=(("T2", 0),))
                    sc.op("act", lambda e: e.activation(T2[0:64, 0:254], T2[0:64, 0:254], AF.Exp, scale=-0.5),
                          reads=(("T2", 0),), writes=(("T2", 0),))
                    sc.op("dve", lambda e: e.scalar_tensor_tensor(Kc[0:64, 0:254], PS[7][0:64, 0:254], vt[0:64, V_GKC:V_GKC + 1],
                                                                  T2[0:64, 0:254], ALU.mult, ALU.mult),
                          reads=(("ps", 7), ("T2", 0), "vec"), writes=("Kcd",))
                else:
                    for g in range(2):
                        for half in range(2):
                            sc.op("pe", lambda e, g=g, half=half: e.matmul(
                                PS[7][0:127, 256 + g * 64:256 + (g + 1) * 64],
                                hidT[:, 508 + half * 254 + g * 127:508 + half * 254 + (g + 1) * 127], w2v_[1][:, half, :],
                                start=(half == 0), stop=(half == 1)),
                                reads=(("hidT", 1), "w2vb"), writes=(("ps", 7),))
                    sc.op("act", lambda e: e.activation(vctok[0:127, :].rearrange("p (g x) -> p g x", g=2)[:, :, 0:64],
                                                        PS[7][0:127, 256:384].rearrange("p (g d) -> p g d", g=2), AF.Copy),
                          reads=(("ps", 7),), writes=("vctok",))
            sc.fence([("sz", p_, tg) for p_ in range(4) for tg in range(4)])

        extras = []
        for kv in range(2):
            for tg in range(4):
                def fkc(tg, P, pk, kv=kv):
                    sc.op("dve", lambda e: e.tensor_copy(KCB[kv][:, tgs(tg)], P[:, :]), reads=(pk,), writes=(("KCB", kv),))
                extras.append((3, kv * 128, 128, tg, fkc))
        for tg in range(4):
            extras.append((3, 384, 128, tg, post_v(0)))
        cgen = compression()
        ei = 0
        csteps = 0
        for c in range(4):
            for slot in (("x", 0), ("x", 1), ("z", 0), "E", ("x", 2), ("z", 1), "E", ("x", 3), ("z", 2), ("z", 3), "E"):
                if slot == "E":
                    add_job(*extras[ei])
                    ei += 1
                else:
                    kind, tg = slot
                    last = (c == 3 and tg == 3)
                    if kind == "x":
                        add_job(0, c * 128, 128, tg, post_x(c), after=((lambda: load_group(2)) if last else None), staged=True)
                    else:
                        add_job(1, c * 128, 128, tg, post_z(c), after=((lambda: load_group(4)) if last else None), staged=True)
                        if last:
                            def lru_done():
                                for _ in cgen:
                                    pass
                                setup_qk()
                            jobs[-1]["after_post"] = lru_done
                if ei > 8 and csteps < 7 and not jobs[-1].get("after_post"):
                    jobs[-1]["after_post"] = (lambda: next(cgen, None))
                    csteps += 1
        for tg in range(4):
            add_job(4, 128, 128, tg, post_v(1))
        for _ in range(1):
            add_job(2, 0, 0, 0, (lambda tg, P, pk: None))
        for tg in range(4):
            add_job(3, 256, 128, tg, post_norm(V_GKS, k_outs(0)), after=((lambda: load_group(5)) if tg == 3 else None), staged=True)
        for p_ in range(4):
            for tg in range(4):
                outs = (lambda p_: (lambda tg: [(Qh[2 * p_][0:64, tgs(tg)], ("Qd", 2 * p_, tg)),
                                                (Qh[2 * p_ + 1][0:64, tgs(tg)], ("Qd", 2 * p_ + 1, tg))]))(p_)
                add_job(2, p_ * 128, 128, tg, post_norm(V_GQ, outs), staged=True)
        for tg in range(4):
            add_job(4, 0, 128, tg, post_norm(V_GKW, k_outs(2)), staged=True)

        for p_ in range(4):
            gi, off = (4, 256 + p_ * 128) if p_ < 2 else (5, (p_ - 2) * 128)
            for tg in range(4):
                def f(tg, P, pk, p_=p_):
                    sc.op("act", lambda e: e.activation(siluz[:, p_ * S + tg * 512:p_ * S + (tg + 1) * 512], P[:, :], AF.Silu),
                          reads=(pk,), writes=(("sz", p_, tg),))
                add_job(gi, off, 128, tg, f)

        for tg in range(4):
            def f(tg, P, pk):
                j = tg % 2
                t2 = T2[0:24, j * 512:(j + 1) * 512]
                sc.op("act", lambda e: e.activation(t2, P[0:24, :], AF.Sigmoid), reads=(pk,), writes=(("T2", j),))
                sc.op("act", lambda e: e.activation(SIG[0:24, tgs(tg)], t2, AF.Copy), reads=(("T2", j),), writes=(("SIG", tg),))
                sc.op("dve", lambda e: e.tensor_tensor(SIG[32:56, tgs(tg)], t2, SIG[0:24, tgs(tg)], ALU.subtract),
                      reads=(("T2", j), ("SIG", tg)), writes=(("SIG", tg),))
            add_job(5, 256, 24, tg, f)

        def emit_proj(ji):
            jb = jobs[ji]
            if jb.get("pre"):
                jb["pre"]()
            b = GBUF[jb["gi"]]
            wv = wst[b].rearrange("p (kc c) -> p kc c", kc=8)
            pi = ji % 4
            tg, off, ncols = jb["tg"], jb["off"], jb["ncols"]
            for kc in range(8 if ncols else 0):
                sc.op("pe", lambda e, kc=kc: e.matmul(PS[pi][0:ncols, :], wv[:, kc, off:off + ncols], xb[kc][:, tg * 512:(tg + 1) * 512],
                                                      start=(kc == 0), stop=(kc == 7)),
                      reads=(("wst", b), ("xb", kc)), writes=(("ps", pi),))
            if ji >= 36 and PDUM:
                sc.op("pe", lambda e: e.matmul(PS[7][:, 0:PDUM], ident, mc[:, 0:PDUM], start=True, stop=True),
                      reads=("cb", "mc"), writes=(("ps", 7),))
            if jb.get("after"):
                jb["after"]()

        LOOK = 2
        MAXS = 9
        PAIR_END = 50
        assert jobs[49]["gi"] == 3 and jobs[48]["ncols"] == 0

        def prep_job(t):
            jb_ = jobs[t]
            if jb_.get("staged"):
                jb_["stages"] = jb_["post"](jb_["tg"], PS[t % 4], ("ps", t % 4))
            else:
                jb_["stages"] = [(lambda jb_=jb_, t=t: jb_["post"](jb_["tg"], PS[t % 4], ("ps", t % 4)))]

        for ji in range(min(LOOK, len(jobs))):
            emit_proj(ji)
        t = 0
        while t < len(jobs) + MAXS:
            ts_ = (t, t + 1) if t < PAIR_END else (t,)
            for t_ in ts_:
                if t_ + LOOK < len(jobs):
                    emit_proj(t_ + LOOK)
            for t_ in ts_:
                if t_ < len(jobs):
                    prep_job(t_)
            done = []
            for s_ in reversed(range(MAXS)):
                for t_ in ts_:
                    j_ = t_ - s_
                    if 0 <= j_ < len(jobs):
                        stg = jobs[j_]["stages"]
                        if s_ < len(stg) and stg[s_] is not None:
                            stg[s_]()
                        if s_ == max(len(stg) - 1, 0):
                            done.append(j_)
            for j_ in sorted(done):
                if jobs[j_].get("after_post"):
                    jobs[j_]["after_post"]()
            t += len(ts_)

        if stop_after == "lru":
            dbg["lru_out"] = (catT[:, 0:4 * S], ("cat", 3), [128, 4 * S], BF16)
            return finish(nc, sc, es, yT, dbg, catT, debug)
        if stop_after == "proj":
            dbg["Q0"] = (Qh[0], ("Qd", 0, 3), [128, S], BF16)
            dbg["Q5"] = (Qh[5], ("Qd", 5, 3), [128, S], BF16)
            dbg["Ks1"] = (Kt[1], ("Kd", 1, 3), [128, S], BF16)
            dbg["Kw0"] = (Kt[2], ("Kd", 2, 3), [128, S], BF16)
            dbg["Kc"] = (Kc[:, :], "Kcd", [128, 254], BF16)
            dbg["vctok"] = (vctok[:, :], "vctok", [128, 256], BF16)
            dbg["vtok"] = (vtok[:, :], ("vtok", 1, 3), [128, 8192], BF16)
            dbg["siluz"] = (siluz[:, :], ("sz", 3, 3), [128, 4 * S], BF16)
            dbg["SIG"] = (SIG[:, :], ("SIG", 3), [128, S], BF16)
            return finish(nc, sc, es, yT, dbg, catT, debug)

        w_outb = A2[:, 0:8192].rearrange("p (kc c) -> p kc c", kc=8)
        PT = [A2[:, 8192 + r * 512:8192 + (r + 1) * 512] for r in range(3)]
        Rt = A2[:, 9728:10752].bitcast(F32)
        Wt = A2[:, 10752:11776].bitcast(F32)
        Tm = A2[:, 11776:12800].bitcast(F32)
        TS = [(Rt, Wt, Tm), (A2[:, 12800:13824].bitcast(F32), A2[:, 13824:14848].bitcast(F32), A2[:, 14848:15872].bitcast(F32))]
        acc = [A3[:, h * 512:(h + 1) * 512] for h in range(8)]
        sc.fence(["w_outb"] + [("PT", r) for r in range(3)] + [("Rt", j) for j in range(2)] + [("Wt", j) for j in range(2)]
                 + [("Tm", j) for j in range(2)] + [("acc", h) for h in range(8)]
                 + [("cat", 4 + p_, c) for p_ in range(4) for c in range(4)]
                 + [("ors", o_, hf_) for o_ in range(3) for hf_ in range(2)] + ["psG", "psI"])
        sc.dma("pool", w_outb, w_out.rearrange("(kc p) c -> p kc c", p=128), (), ("w_outb",), "w_outb")

        for i_ in range(WARM):
            sc.op("pe", lambda e: e.matmul(PS[0][:, :], ident, mc[:, 0:512], start=True, stop=True),
                  reads=("cb", "mc"), writes=(("ps", 0),))
        PS7b = PS[6][:, 256:320].bitcast(BF16)
        selh_v = selh[:, :].rearrange("p (k d) -> p k d", k=24)
        ectr = [0]
        PS6v = PS[6][:, 0:132].rearrange("p (t m) -> p t m", m=33)
        impacc_v = impacc[:, :].rearrange("p (t m) -> p t m", m=32)
        impt_v = impt[:, :].rearrange("p (t m) -> p t m", m=32)
        score_v = score[:, :].rearrange("p (t m) -> p t m", m=32)
        rI = small[:, 8:12]
        m8 = small[:, 16:48]
        gctr = [0]
        ORSB = [3, 4, 7]
        NORS = [2]

        def epilogue(h, c, br, o, last_br):
            cols = slice(c * 512, (c + 1) * 512)
            j = ectr[0] % 2
            ectr[0] += 1
            Rt_, Wt_, Tm_ = TS[j]
            P_ = slice(0, 64)
            R_ = slice(64, 128)
            ORS = PS[ORSB[o]]
            use_dve = br in DVE_RECIP and c >= DVE_RECIP_MINC and not (br == 0 and c == 0)
            if use_dve:
                Osrc, okeys = ORS[P_, :], (("ors", o, 0),)
            else:
                Osrc, okeys = T0[P_, j * 512:(j + 1) * 512], (("T0", j),)
                sc.op("dve", lambda e: e.tensor_copy(Osrc, ORS[P_, :]), reads=(("ors", o, 0),), writes=okeys)
            sc.op("pe", lambda e: e.matmul(PS[5][:, :], selh_v[:, 3 * h + (0, 2, 1)[br], :], SIG[:, cols], start=True, stop=True),
                  reads=(("SIG", c), "selh"), writes=("psG",))
            if br == 0 and c == 0:
                sc.op("act", lambda e: e.activation(Rt_[R_, :], ORS[R_, :], AF.Ln, bias=cf[R_, CF_CV + 2:CF_CV + 3], scale=1.0),
                      reads=(("ors", o, 1), "cf"), writes=(("Rt", j),))
            elif not (br in DVE_RECIP and c >= DVE_RECIP_MINC):
                sc.op("act", lambda e: e.activation(Rt_[R_, :], ORS[R_, :], AF.Ln), reads=(("ors", o, 1),), writes=(("Rt", j),))
            if br in DVE_RECIP and c >= DVE_RECIP_MINC and not (br == 0 and c == 0):
                sc.op("dve", lambda e: e.reciprocal(Rt_[R_, :], ORS[R_, :]), reads=(("ors", o, 1),), writes=(("Rt", j),))
            else:
                sc.op("act", lambda e: e.activation(Rt_[R_, :], Rt_[R_, :], AF.Exp, scale=-1.0), reads=(("Rt", j),), writes=(("Rt", j),))
            sc.op("dve", lambda e: e.tensor_tensor(Wt_[P_, :], Rt_[R_, :], PS[5][R_, :], ALU.mult),
                  reads=(("Rt", j), "psG"), writes=(("Wt", j),))
            if br not in branches:
                pass
            elif br == branches[0]:
                sc.op("dve", lambda e: e.tensor_tensor(acc[h][P_, :], Osrc, Wt_[P_, :], ALU.mult),
                      reads=okeys + (("Wt", j),), writes=(("acc", h),))
            else:
                sc.op("dve", lambda e: e.tensor_tensor(Tm_[P_, :], Osrc, Wt_[P_, :], ALU.mult),
                      reads=okeys + (("Wt", j),), writes=(("Tm", j),))
                sc.op("pool", lambda e: e.tensor_tensor(acc[h][P_, :], acc[h][P_, :], Tm_[P_, :], ALU.add),
                      reads=(("acc", h), ("Tm", j)), writes=(("acc", h),))
            if last_br:
                pr = h // 2
                cs = slice((4 + pr) * S + c * 512, (4 + pr) * S + (c + 1) * 512)
                zs = slice(pr * S + c * 512, pr * S + (c + 1) * 512)
                if h % 2 == 0:
                    sc.op("pool", lambda e: e.tensor_tensor(catT[P_, cs], acc[h][P_, :], siluz[P_, zs], ALU.mult),
                          reads=(("acc", h), ("sz", pr, c)), writes=(("cat", 4 + pr, c),))
                else:
                    sc.op("pool", lambda e: e.tensor_copy(acc[h][R_, :], acc[h][P_, :]), reads=(("acc", h),), writes=(("acc", h),))
                    sc.op("pool", lambda e: e.tensor_tensor(catT[R_, cs], acc[h][R_, :], siluz[R_, zs], ALU.mult),
                          reads=(("acc", h), ("sz", pr, c)), writes=(("cat", 4 + pr, c),))

        xre = [A3[:, 4096:4608], A3[:, 4608:5120]]
        ost = [A3[:, 5120:5632], A3[:, 5632:6144]]
        sc.fence([("ost", 0), ("ost", 1), ("xre", 0), ("xre", 1)])
        octr = [0]

        def outproj_slots(tc, banks, gap, nbuf=2):
            tcs = slice(tc * 512, (tc + 1) * 512)
            if nbuf > 2:
                for i_ in range(nbuf):
                    sc.dma("sp", xre[i_], xT[i_ * 128:(i_ + 1) * 128, tcs], (), (("xre", i_),), "xre%d" % i_)
            for cc in range(8):
                i2 = (octr[0] % 2) if nbuf == 2 else cc
                bk = banks[octr[0] % len(banks)]
                octr[0] += 1
                bkeys = (("ors", 2, 0), ("ors", 2, 1)) if bk == 7 else (("ps", bk),)
                for kc in range(8):
                    def mm(kc=kc, cc=cc, i2=i2, bk=bk):
                        if kc == 0 and nbuf == 2:
                            sc.dma("sp", xre[i2], xT[cc * 128:(cc + 1) * 128, tcs], (), (("xre", i2),), "xre%d" % i2)
                        rkeys = (("cat", kc),) if kc < 4 else (("cat", kc, tc),)
                        sc.op("pe", lambda e: e.matmul(PS[bk][:, :], w_outb[:, kc, cc * 128:(cc + 1) * 128],
                                                       catT[:, kc * S + tc * 512:kc * S + (tc + 1) * 512],
                                                       start=(kc == 0), stop=(kc == 7)),
                              reads=("w_outb",) + rkeys, writes=bkeys)
                    yield mm

                def ev(cc=cc, i2=i2, bk=bk):
                    sc.op("dve", lambda e: e.tensor_tensor(ost[i2], PS[bk][:, :], xre[i2], ALU.add),
                          reads=bkeys + (("xre", i2),), writes=(("ost", i2),))
                    sc.dma("sp", yT[cc * 128:(cc + 1) * 128, tcs], ost[i2], (("ost", i2),), (), "ost%d" % i2)
                yield ev
                for _ in range(gap):
                    yield None

        for c in range(4):
            cols = slice(c * 512, (c + 1) * 512)
            NORS[0] = 3 if c < 2 else 2
            if c < 2:
                filler = iter(())
            elif c == 2:
                filler = outproj_slots(0, [7], 6)
            else:
                def two_():
                    for a_ in outproj_slots(1, [7], 3):
                        yield a_
                    for a_ in outproj_slots(2, [7], 3):
                        yield a_
                filler = two_()
            units = []

            def branch_units(kind, br, h):
                ul = []
                if kind == "w":
                    for b_ in range(4):
                        kt = 4 * c - 4 + b_
                        if kt >= 0:
                            ul.append(dict(kt=kt, qlo=0, qhi=128 * (b_ + 1), mask=("atri", 128 * b_)))
                else:
                    for kt in range(4 * c):
                        ul.append(dict(kt=kt, qlo=0, qhi=512, mask=None))
                for a_ in range(4):
                    ul.append(dict(kt=4 * c + a_, qlo=128 * a_, qhi=512, mask=("tri", 128 * a_)))
                merged = []
                cur = None
                for i_, u in enumerate(ul):
                    n_ = u["qhi"] - u["qlo"]
                    u["sfirst"] = (i_ == 0)
                    u["slast"] = (i_ == len(ul) - 1)
                    if cur is not None and cur["tot"] + n_ <= 512:
                        u["off"] = cur["tot"]
                        cur["subs"].append(u)
                        cur["tot"] += n_
                    else:
                        u["off"] = 0
                        cur = dict(subs=[u], tot=n_)
                        merged.append(cur)
                for i_, m in enumerate(merged):
                    m.update(kind=kind, h=h, br=br, first=(i_ == 0), last=(i_ == len(merged) - 1))
                return merged

            for h in range(8):
                units.append(dict(kind="c", h=h, first=True, last=True, br=0))
                units += branch_units("w", 1, h)
            for h in range(8):
                units += branch_units("s", 2, h)

            def emit_S(u, r):
                h = u["h"]
                g = h // 4
                if u["kind"] == "c":
                    sc.op("pe", lambda e: e.matmul(PS[r][0:127, :], Kc[0:101, g * 127:(g + 1) * 127], Qh[h][0:101, cols],
                                                   start=True, stop=False),
                          reads=("Kcd", "Kcr", ("Qd", h, c), ("Qs", h, c), ("Qr", h)), writes=(("ps", r),))
                    sc.op("pe", lambda e: e.matmul(PS[r][0:127, :], ident[0:127, 0:127], mc[0:127, cols], start=False, stop=True),
                          reads=("cb", "mc"), writes=(("ps", r),))
                    return
                ki = g if u["kind"] == "s" else 2 + g
                for sb_ in u["subs"]:
                    kt, qlo, qhi, off = sb_["kt"], sb_["qlo"], sb_["qhi"], sb_["off"]
                    n_ = qhi - qlo
                    sc.op("pe", lambda e: e.matmul(PS[r][:, off:off + n_], Kt[ki][0:101, kt * 128:(kt + 1) * 128],
                                                   Qh[h][0:101, c * 512 + qlo:c * 512 + qhi], start=True, stop=(sb_["mask"] is None)),
                          reads=(("Kd", ki, kt // 4), ("Kr", ki), ("Qd", h, c), ("Qs", h, c), ("Qr", h)), writes=(("ps", r),))
                    if sb_["mask"] is not None:
                        mt, mlo = sb_["mask"]
                        msk = TRI if mt == "tri" else ATRI
                        mo = off + (mlo - qlo)
                        sc.op("pe", lambda e: e.matmul(PS[r][:, mo:mo + 128], ident, msk, start=False, stop=True),
                              reads=("cb",), writes=(("ps", r),))

            def emit_exp(u, r):
                h = u["h"]
                if u["kind"] == "c":
                    sc.op("act", lambda e: e.activation(PT[r][0:127, :], PS[r][0:127, :], AF.Exp, scale=0.125),
                          reads=(("ps", r),), writes=(("PT", r),))
                    return
                tot = u["tot"]
                sc.op("act", lambda e: e.activation(PT[r][:, 0:tot], PS[r][:, 0:tot], AF.Exp, scale=0.125),
                      reads=(("ps", r),), writes=(("PT", r),))

            def emit_PV(u, r):
                h = u["h"]
                g = h // 4
                pb = (h % 2) * 64
                rb = 64 - pb
                if u["first"]:
                    gctr[0] += 1
                u["o"] = gctr[0] % NORS[0]
                o = u["o"]
                ORS = PS[ORSB[o]]
                if u["kind"] == "c":
                    sc.op("pe", lambda e: e.matmul(ORS[:, :], vctok[0:127, g * 128:(g + 1) * 128], PT[r][0:127, :],
                                                   start=True, stop=True),
                          reads=(("PT", r), "vctok"), writes=(("ors", o, 0), ("ors", o, 1)))
                    for tt in range(4):
                        sc.op("pe", lambda e, tt=tt: e.matmul(PS[6][:, tt * 33:(tt + 1) * 33], PT[r][0:127, tt * 128:(tt + 1) * 128],
                                                              ova[0:127, 0:33], start=True, stop=True),
                              reads=(("PT", r), "ova"), writes=("psI",))
                    jh = h % 4
                    sc.op("dve", lambda e: e.tensor_scalar(rI, PS6v[:, :, 32], 1e-30, None, ALU.max),
                          reads=("psI",), writes=("rI",))
                    sc.op("dve", lambda e: e.reciprocal(rI, rI), reads=("rI",), writes=("rI",))
                    rIb = rI.unsqueeze(2).to_broadcast([128, 4, 32])
                    if jh == 0:
                        sc.op("dve", lambda e: e.tensor_tensor(impacc_v, PS6v[:, :, 0:32], rIb, ALU.mult),
                              reads=("psI", "rI"), writes=("impacc",))
                    else:
                        sc.op("dve", lambda e: e.tensor_tensor(impt_v, PS6v[:, :, 0:32], rIb, ALU.mult),
                              reads=("psI", "rI"), writes=("impt",))
                        sc.op("dve", lambda e: e.tensor_tensor(impacc[:, :], impacc[:, :], impt[:, :], ALU.add),
                              reads=("impacc", "impt"), writes=("impacc",))
                    if jh == 3:
                        a0 = CF_A + c * 128
                        b0 = CF_B + c * 128
                        sc.op("dve", lambda e: e.tensor_tensor(score[:, :], impacc[:, :], cf[:, a0:a0 + 128], ALU.mult),
                              reads=("impacc", "cf"), writes=("score",))
                        sc.op("dve", lambda e: e.tensor_tensor(score[:, :], score[:, :], cf[:, b0:b0 + 128], ALU.add),
                              reads=("score", "cf"), writes=("score",))
                        for tt in range(4):
                            sc.op("dve", lambda e, tt=tt: e.max(m8[:, tt * 8:(tt + 1) * 8], score[:, tt * 32:(tt + 1) * 32]),
                                  reads=("score",), writes=("m8",))
                        for tt in range(4):
                            sc.op("dve", lambda e, tt=tt: e.tensor_scalar(impt[:, tt * 32:(tt + 1) * 32], score[:, tt * 32:(tt + 1) * 32],
                                                                          m8[:, tt * 8 + 7:tt * 8 + 8], 1.0, ALU.is_ge, ALU.subtract),
                                  reads=("score", "m8"), writes=("impt",))
                        sg_ = selt[:, g * 128:(g + 1) * 128]
                        sc.op("dve", lambda e: e.tensor_scalar(sg_, impt[:, :], -NEGM, None, ALU.mult),
                              reads=("impt",), writes=(("selt", g),))

                        def sel_pe(g=g, sg_=sg_):
                            sc.op("pe", lambda e: e.transpose(PS7b[:, 0:128], sg_, ident),
                                  reads=(("selt", g), "cb"), writes=("psI",))
                            for tt in range(4):
                                sc.op("dve", lambda e, tt=tt: e.tensor_copy(Qh[4 * g][64:96, c * 512 + tt * 128:c * 512 + (tt + 1) * 128],
                                                                            PS7b[tt * 32:(tt + 1) * 32, 0:128]),
                                      reads=("psI",), writes=(("Qs", 4 * g, c),))
                            for k_ in range(1, 4):
                                sc.op("pool", lambda e, k_=k_: e.tensor_copy(Qh[4 * g + k_][64:96, cols], Qh[4 * g][64:96, cols]),
                                      reads=(("Qs", 4 * g, c),), writes=(("Qs", 4 * g + k_, c),))
                        deferred.append((u["idx"] + 6, sel_pe))
                    return
                which = 0 if u["kind"] == "s" else 1
                for sb_ in u["subs"]:
                    kt, qlo, qhi, off = sb_["kt"], sb_["qlo"], sb_["qhi"], sb_["off"]
                    n_ = qhi - qlo
                    vb = ((which * 16 + kt) * 2 + g) * 128
                    sc.op("pe", lambda e: e.matmul(ORS[:, qlo:qhi], vtok[:, vb:vb + 128], PT[r][:, off:off + n_],
                                                   start=sb_["sfirst"], stop=sb_["slast"], skip_group_check=True),
                          reads=(("PT", r), ("vtok", which, kt // 4)), writes=(("ors", o, 0), ("ors", o, 1)))

            emit_S(units[0], 0)
            emit_S(units[1], 1)
            pending = []
            deferred = []
            for i_, u in enumerate(units):
                u["idx"] = i_
                r = i_ % 3
                if i_ + 2 < len(units):
                    emit_S(units[i_ + 2], (i_ + 2) % 3)
                emit_exp(u, r)
                keep = []
                for ep in pending:
                    if ep[0] <= i_:
                        epilogue(*ep[1])
                    else:
                        keep.append(ep)
                pending = keep
                for d_ in [d_ for d_ in deferred if d_[0] <= i_]:
                    d_[1]()
                    deferred.remove(d_)
                emit_PV(u, r)
                act_ = next(filler, "done")
                if act_ == "done" or c < 2:
                    if NDUM and c >= 2:
                        sc.op("pe", lambda e: e.matmul(PS[7][:, 0:NDUM], ident, mc[:, 0:NDUM], start=True, stop=True),
                              reads=("cb", "mc"), writes=(("ors", 2, 0), ("ors", 2, 1)))
                elif act_ is not None:
                    act_()
                if u["last"]:
                    j_ = i_
                    while not units[j_]["first"]:
                        j_ -= 1
                    nxt_long = (i_ + 1 < len(units)) and not (units[i_ + 1]["kind"] == "c")
                    nxt_long = (i_ + 1 < len(units)) and not (units[i_ + 1]["kind"] == "c")
                    pending.append((i_ + (2 if nxt_long else 1), (u["h"], c, u["br"], units[j_]["o"], u["br"] == 2)))
            for ep in pending:
                epilogue(*ep[1])
            for d_ in deferred:
                d_[1]()
            for act_ in filler:
                if act_:
                    act_()

        if stop_after == "attn":
            dbg["cat"] = (catT[:, :], ("cat", 7, 3), [128, 8 * S], BF16)
            return finish(nc, sc, es, yT, dbg, catT, debug)

        xre = [A1[:, i_ * 512:(i_ + 1) * 512] for i_ in range(8)]
        ost = [A1[:, 4096 + i_ * 512:4096 + (i_ + 1) * 512] for i_ in range(8)]
        sc.fence([("xre", i_) for i_ in range(8)] + [("ost", i_) for i_ in range(8)])
        for act_ in outproj_slots(3, [0, 1, 2], 0, nbuf=8):
            if act_:
                act_()
        sc.wait_all_dma("sp", ["ost%d" % i_ for i_ in range(8)])
    return nc


def finish(nc, sc, es, yT, dbg, catT, debug):
    for name, (ap, key, shape, dt) in dbg.items():
        d = nc.dram_tensor("dbg_" + name, list(shape), dt, kind="ExternalOutput").ap()
        sc.dma("sp", d, ap, (key,), (), "dbg_" + name)
    sc.wait_all_dma("sp", [k for k in sc.dsems if k.startswith("dbg_")])
    return nc


def prep_shared(inp):
    f = np.float32
    m = {}
    m["w_in"] = np.ascontiguousarray(inp["w_in"][0], dtype=f)
    m["w_out"] = np.ascontiguousarray(inp["w_out"][0], dtype=f)
    vec = np.zeros((128, V_W), f)
    vec[:, V_G:V_G + 8] = inp["norm_g"][0].reshape(8, 128).T
    cw = inp["conv_w"][0]
    vec[:, V_CW:V_CW + 16] = cw.reshape(4, 4, 128).transpose(2, 1, 0).reshape(128, 16)
    for col, name in ((V_CB, "conv_b"), (V_BA, "lru_ba"), (V_BI, "lru_bi"), (V_LAM, "lru_lambda")):
        vec[:, col:col + 4] = inp[name][0].reshape(4, 128).T
    for col, name in ((V_GQ, "q_norm_g"), (V_GKS, "ks_norm_g"), (V_GKW, "kw_norm_g"), (V_GKC, "kc_norm_g")):
        vec[:, col] = np.tile(inp[name][0], 2)
    m["vec"] = vec
    for name, key in (("wa", "lru_wa"), ("wi", "lru_wi")):
        w = inp[key][0]
        bd = np.zeros((128, 4, 128), f)
        for n_ in range(8):
            bd[(n_ % 2) * 64:(n_ % 2) * 64 + 64, n_ // 2, (n_ % 2) * 64:(n_ % 2) * 64 + 64] = w[n_]
        m[name] = np.ascontiguousarray(bd.reshape(128, 512))
    pos = inp["cmp_pos"][0]
    m["posT2"] = np.ascontiguousarray(pos.reshape(16, 2, 64).transpose(1, 2, 0).reshape(128, 16), dtype=f)
    for name, key in (("w1k", "cmp_k_w1"), ("w1v", "cmp_v_w1")):
        w = inp[key][0]
        m[name] = np.ascontiguousarray(w.reshape(16, 2, 64, 256).transpose(1, 2, 0, 3).reshape(128, 4096), dtype=f)
    for name, key in (("w2k", "cmp_k_w2"), ("w2v", "cmp_v_w2")):
        w = inp[key][0]
        m[name] = np.ascontiguousarray(w.reshape(2, 128, 64).transpose(1, 0, 2).reshape(128, 128), dtype=f)
    m.update(host_consts())
    return m


def kernel(**inputs):
    inp = {k: np.asarray(v) for k, v in inputs.items()}
    shared = prep_shared(inp)
    nc = build()
    in_maps = []
    for b in range(8):
        mm = dict(shared)
        mm["xT"] = np.ascontiguousarray(inp["x"][b].T, dtype=np.float32)
        in_maps.append(mm)
    res = run_bass_kernel_spmd(nc, in_maps, core_ids=list(range(8)))
    out = np.stack([np.ascontiguousarray(r["yT"].T) for r in res.results], axis=0)
    return out.astype(np.float32)
```

```python
import numpy as np
import ml_dtypes
from contextlib import ExitStack
import concourse.bass as bass
import concourse.mybir as mybir
from concourse.bass_utils import run_bass_kernel_spmd

F32 = mybir.dt.float32
BF16 = mybir.dt.bfloat16
AF = mybir.ActivationFunctionType
ALU = mybir.AluOpType
S = 2048
D = 1024
DIN = 2840
NEGM = -240000.0
EPS = 1e-6
WARM = 0
NDUM = 0
PDUM = 0
DVE_RECIP = (2,)
DVE_RECIP_MINC = 2
BF = ml_dtypes.bfloat16


class Sched:
    def __init__(self, nc, es):
        self.nc = nc
        self.es = es
        self.E = {}
        for name, obj in (("pe", nc.tensor), ("dve", nc.vector), ("act", nc.scalar),
                          ("pool", nc.gpsimd), ("sp", nc.sync)):
            sem = es.enter_context(nc.semaphore("s_" + name))
            self.E[name] = {"obj": obj, "sem": sem, "cnt": 0, "waited": {}, "name": name}
        self.lastw = {}
        self.readers = {}
        self.dsems = {}

    def _deps(self, E, reads, writes):
        toks = []
        me = E["name"]
        for r in reads:
            t = self.lastw.get(r)
            if t is not None and not (t[2] == "pe" and me == "pe"):
                toks.append(t)
            if is_psum_key(r):
                for t in self.readers.get(r, ()):
                    if t[2] != me:
                        toks.append(t)
        for w in writes:
            t = self.lastw.get(w)
            if t is not None and t[2] != me:
                toks.append(t)
            for t in self.readers.get(w, ()):
                if t[2] != me:
                    toks.append(t)
        need = {}
        for t in toks:
            k = id(t[0])
            if k not in need or need[k][1] < t[1]:
                need[k] = (t[0], t[1])
        for k, (sem, val) in need.items():
            if E["waited"].get(k, 0) < val:
                E["obj"].wait_ge(sem, val)
                E["waited"][k] = val

    def _commit(self, tok, reads, writes):
        for w in writes:
            self.lastw[w] = tok
            self.readers[w] = []
        for r in reads:
            self.readers.setdefault(r, []).append(tok)

    def op(self, eng, fn, reads=(), writes=()):
        E = self.E[eng]
        self._deps(E, reads, writes)
        ins = fn(E["obj"])
        E["cnt"] += 1
        ins.then_inc(E["sem"], 1)
        self._commit([E["sem"], E["cnt"], eng], reads, writes)

    def dma(self, q, out, in_, reads, writes, dkey):
        E = self.E[q]
        self._deps(E, reads, writes)
        if dkey not in self.dsems:
            self.dsems[dkey] = [self.es.enter_context(self.nc.semaphore("d_" + dkey)), 0, []]
        ds = self.dsems[dkey]
        ins = E["obj"].dma_start(out=out, in_=in_)
        ds[1] += 16
        ins.then_inc(ds[0], 16)
        tok = [ds[0], ds[1], "dma"]
        ds[2].append(tok)
        self._commit(tok, reads, writes)

    def finalize_group(self, dkey):
        ds = self.dsems[dkey]
        for t in ds[2]:
            t[1] = ds[1]

    def fence(self, keys):
        toks = [[E["sem"], E["cnt"], "fence_" + n] for n, E in self.E.items() if E["cnt"] > 0]
        for k in keys:
            self.readers[k] = list(toks) + list(self.readers.get(k, []))

    def wait_all_dma(self, q, dkeys):
        E = self.E[q]
        for dk in dkeys:
            ds = self.dsems[dk]
            E["obj"].wait_ge(ds[0], ds[1])


def is_psum_key(k):
    if isinstance(k, tuple):
        return k[0] in ("ps", "ors")
    return k in ("psG", "psI", "psT")


def slopes():
    return [2.0 ** (-(h + 1)) for h in range(8)]


def host_consts():
    c = {}
    ident = np.eye(128, dtype=np.float32)
    blockones = np.kron(np.eye(2), np.ones((64, 64))).astype(np.float32)
    ones = np.ones((128, 128), np.float32)
    i = np.arange(128)[:, None]
    j = np.arange(128)[None, :]
    tri = np.where(j >= i, 0.0, NEGM).astype(np.float32)
    atri = np.where(j < i, 0.0, NEGM).astype(np.float32)
    c["cb"] = np.concatenate([ident, blockones, ones, tri, atri], axis=1).astype(BF)
    n = np.arange(128)[:, None]
    t = np.arange(S)[None, :]
    c["mc"] = np.where(t >= 16 * n + 31, 0.0, NEGM).astype(BF)
    selh = np.zeros((128, 24, 128), np.float32)
    for k in range(24):
        selh[k, k, :] = 1.0
        selh[32 + k, k, :] = 1.0
    c["selh"] = selh.reshape(128, 24 * 128).astype(BF)
    cs = np.arange(127) * 16
    ce = cs + 32
    ss = np.arange(32) * 64
    se = ss + 64
    ov = np.clip(np.minimum(ce[:, None], se[None, :]) - np.maximum(cs[:, None], ss[None, :]), 0, None) / 32.0
    ova = np.zeros((128, 33), np.float32)
    ova[:127, :32] = ov
    ova[:127, 32] = 1.0
    c["ova"] = ova.astype(BF)
    k = np.arange(S)
    krs = np.zeros((37, S), np.float32)
    krs[:32] = (k[None, :] // 64 == np.arange(32)[:, None])
    krs[32] = 1.0
    krs[33] = 1.0
    krs[34] = k % 128
    krs[35] = 1.0
    krs[36] = k // 128
    krw = krs.copy()
    krw[:32] = 0.0
    krc = np.zeros((37, 254), np.float32)
    krc[32] = 1.0
    krc[33] = 1.0
    krc[34] = 16.0 * (np.arange(254) % 127)
    krc[35] = 1.0
    krc[36] = 31.0 / 128.0
    c["krs"] = krs.astype(BF)
    c["krw"] = krw.astype(BF)
    c["krc"] = krc.astype(BF)
    jj = k % 512
    qrows = np.zeros((8, 5, S), np.float32)
    for h, sl in enumerate(slopes()):
        qrows[h, 0] = -8.0 * sl * (jj % 256)
        qrows[h, 1] = -8.0 * sl * 256.0 * (jj // 256)
        qrows[h, 2] = 8.0 * sl
        qrows[h, 3] = -8.0 * sl * 512.0 * (k // 512)
        qrows[h, 4] = 8.0 * sl * 128.0
    c["qrows"] = qrows.astype(BF)
    bsw = np.zeros((128, 128), np.float32)
    bc = np.zeros((128, 32), np.float32)
    for h, sl in enumerate(slopes()):
        for dl in range(-3, 13):
            bsw[:, h * 16 + dl + 3] = -sl * 128.0 * dl
        for cc in range(4):
            bc[:, h * 4 + cc] = sl * (16.0 * np.arange(128) + 31.0 - 512.0 * cc)
    A = np.zeros((128, 16, 32), np.float32)
    Bm = np.zeros((128, 16, 32), np.float32)
    for tt in range(16):
        tpos = tt * 128 + np.arange(128)
        cur = (tpos // 64)[:, None]
        m = np.arange(32)[None, :]
        forced = (m == 0) | (m == cur) | (m == cur - 1)
        A[:, tt, :] = np.where(forced, 0.0, np.where(m <= cur, 1.0, 0.0))
        Bm[:, tt, :] = np.where(forced, 1.0e4, np.where(m <= cur, 0.0, -1.0))
    cv = np.zeros((128, 3), np.float32)
    cv[:, 0] = EPS
    cv[:, 1] = 1.0
    cv[:, 2] = 1e-30
    c["cf"] = np.concatenate([bsw, bc, A.reshape(128, 512), Bm.reshape(128, 512), cv], axis=1).astype(np.float32)
    return c


CF_BSW, CF_BC, CF_A, CF_B, CF_CV = 0, 128, 160, 672, 1184
CF_W = 1187
V_G, V_CW, V_CB, V_BA, V_BI, V_LAM, V_GQ, V_GKS, V_GKW, V_GKC = 0, 8, 24, 28, 32, 36, 40, 41, 42, 43
V_W = 44


def build(debug=(), stop_after=None, branches=(0, 1, 2)):
    nc = bass.Bass("TRN2", target_bir_lowering=False)
    es = ExitStack()
    with es:
        sc = Sched(nc, es)

        def din(name, shape, dt=F32):
            return nc.dram_tensor(name, list(shape), dt, kind="ExternalInput").ap()

        xT = din("xT", [D, S])
        w_in = din("w_in", [D, DIN])
        w_out = din("w_out", [D, D])
        vec = din("vec", [128, V_W])
        wa_d = din("wa", [128, 512])
        wi_d = din("wi", [128, 512])
        pos_d = din("posT2", [128, 16])
        w1k_d = din("w1k", [128, 4096])
        w1v_d = din("w1v", [128, 4096])
        w2k_d = din("w2k", [128, 128])
        w2v_d = din("w2v", [128, 128])
        cb_d = din("cb", [128, 640], BF16)
        mc_d = din("mc", [128, S], BF16)
        selh_d = din("selh", [128, 3072], BF16)
        ova_d = din("ova", [128, 33], BF16)
        krs_d = din("krs", [37, S], BF16)
        krw_d = din("krw", [37, S], BF16)
        krc_d = din("krc", [37, 254], BF16)
        qrows_d = din("qrows", [8, 5, S], BF16)
        cf_d = din("cf", [128, CF_W])
        yT = nc.dram_tensor("yT", [D, S], F32, kind="ExternalOutput").ap()

        def sb(name, shape, dt):
            return es.enter_context(nc.sbuf_tensor(name, list(shape), dt))

        def ps(name):
            return es.enter_context(nc.psum_tensor(name, [128, 512], F32))

        A1 = sb("A1", [128, 12288], F32)
        A2 = sb("A2", [128, 16384], BF16)
        A3 = sb("A3", [128, 6144], F32)
        Kc = sb("Kc", [128, 254], BF16)
        vctok = sb("vctok", [128, 256], BF16)
        vtok = sb("vtok", [128, 2 * 16 * 256], BF16)
        siluz = sb("siluz", [128, 4 * S], BF16)
        SIG = sb("SIG", [128, S], BF16)
        catT = sb("catT", [128, 8 * S], BF16)
        hidT = sb("hidT", [128, 2 * 508], BF16)
        T0 = sb("T0", [128, 1024], F32)
        T1 = sb("T1", [128, 1024], BF16)
        T2 = sb("T2", [128, 1024], F32)
        cb = sb("cbs", [128, 640], BF16)
        mc = sb("mcs", [128, S], BF16)
        selh = sb("selhs", [128, 3072], BF16)
        ova = sb("ovas", [128, 33], BF16)
        cf = sb("cfs", [128, CF_W], F32)
        vt = sb("vecs", [128, V_W], F32)
        wab = sb("wab", [128, 512], BF16)
        wib = sb("wib", [128, 512], BF16)
        posb = sb("posb", [128, 16], BF16)
        w2kb = sb("w2kb", [128, 128], BF16)
        w2vb = sb("w2vb", [128, 128], BF16)
        small = sb("small", [128, 64], F32)
        impacc = sb("impacc", [128, 128], F32)
        impt = sb("impt", [128, 128], F32)
        score = sb("score", [128, 128], F32)
        selt = sb("selt", [128, 256], BF16)
        VT = T0[:, :].bitcast(BF16)
        PS = [ps("ps%d" % i) for i in range(8)]

        ident = cb[:, 0:128]
        blockones = cb[:, 128:256]
        ones = cb[:, 256:384]
        TRI = cb[:, 384:512]
        ATRI = cb[:, 512:640]
        eps_ap = cf[:, CF_CV:CF_CV + 1]
        one_ap = cf[:, CF_CV + 1:CF_CV + 2]

        B = [A1[:, i * 2048:(i + 1) * 2048] for i in range(5)]
        XCb = [A1[:, 10240 + i * 1024:10240 + (i + 1) * 1024].bitcast(BF16) for i in range(2)]
        Qh = [A1[:, h * 1024:(h + 1) * 1024].bitcast(BF16) for h in range(8)]
        Kt = [A1[:, 8192 + i * 1024:8192 + (i + 1) * 1024].bitcast(BF16) for i in range(4)]
        xb = [A2[:, kc * 2048:(kc + 1) * 2048] for kc in range(8)]
        rstd = A3[:, 0:2048]
        wst = [A3[:, 2048 + b * 2048:2048 + (b + 1) * 2048].bitcast(BF16) for b in range(2)]
        KCB = [siluz[:, 0:2048], siluz[:, 2048:4096]]
        KB2 = [siluz[:, 4096:6144], siluz[:, 6144:8192]]
        w1kb = catT[:, 4 * S:6 * S]
        w1vb = catT[:, 6 * S:8 * S]

        def cload(q, dst, src, key):
            sc.dma(q, dst, src, (), (key,), "c_" + q)

        cload("sp", cb[:], cb_d, "cb")
        cload("sp", cf[:], cf_d, "cf")
        cload("sp", vt[:], vec, "vec")
        cload("sp", mc[:], mc_d, "mc")
        cload("sp", selh[:], selh_d, "selh")
        cload("sp", ova[:], ova_d, "ova")
        cload("pool", wab[:], wa_d, "wab")
        cload("pool", wib[:], wi_d, "wib")
        cload("pool", posb[:], pos_d, "posb")
        cload("pool", w2kb[:], w2k_d, "w2kb")
        cload("pool", w2vb[:], w2v_d, "w2vb")
        cload("pool", w1kb, w1k_d, "w1kb")
        cload("pool", w1vb, w1v_d, "w1vb")
        sc.finalize_group("c_sp")
        sc.finalize_group("c_pool")
        CONST = ("cb", "cf", "vec")

        def tgs(tg):
            return slice(tg * 512, (tg + 1) * 512)

        def rfast(e, out, in_):
            return e.reciprocal_approx_fast(out, in_)

        xTv = xT.rearrange("(kc p) t -> kc p t", p=128)
        Xs = [A1[:, kc * 2048:(kc + 1) * 2048] for kc in range(6)] + [A3[:, 2048:4096], A3[:, 4096:6144]]

        def xkeys(kc):
            if kc < 5:
                return tuple(("B", kc, tg) for tg in range(4))
            if kc == 5:
                return ("X5",)
            return (("wst", kc - 6),)

        for kc in range(8):
            sc.dma("sp", Xs[kc], xTv[kc], (), xkeys(kc), "xs%d" % kc)
        for kc in range(8):
            for tg in range(4):
                j = (kc * 4 + tg) % 2
                t1 = T1[:, j * 512:(j + 1) * 512]
                sc.op("act", lambda e: e.activation(t1, Xs[kc][:, tgs(tg)], AF.Square),
                      reads=xkeys(kc), writes=(("T1", j),))
                sc.op("pe", lambda e: e.matmul(PS[tg][:, :], ones, t1, start=(kc == 0), stop=(kc == 7)),
                      reads=(("T1", j), "cb"), writes=(("ps", tg),))
        for tg in range(4):
            sc.op("act", lambda e: e.activation(rstd[:, tgs(tg)], PS[tg][:, :], AF.Ln, bias=eps_ap, scale=1.0 / D),
                  reads=(("ps", tg), "cf"), writes=(("rstd", tg),))
            sc.op("act", lambda e: e.activation(rstd[:, tgs(tg)], rstd[:, tgs(tg)], AF.Exp, scale=-0.5),
                  reads=(("rstd", tg),), writes=(("rstd", tg),))
        for kc in (6, 7, 0, 1, 2, 3, 4, 5):
            for tg in range(4):
                sc.op("dve", lambda e: e.scalar_tensor_tensor(xb[kc][:, tgs(tg)], Xs[kc][:, tgs(tg)], vt[:, V_G + kc:V_G + kc + 1],
                                                              rstd[:, tgs(tg)], ALU.mult, ALU.mult),
                      reads=xkeys(kc) + (("rstd", tg), "vec"), writes=(("xb", kc),))

        dbg = {}
        if stop_after == "p1":
            dbg["rstd"] = (rstd, ("rstd", 3), [128, S], F32)
            dbg["xb0"] = (xb[0], ("xb", 0), [128, S], BF16)
            dbg["xb7"] = (xb[7], ("xb", 7), [128, S], BF16)
            return finish(nc, sc, es, yT, dbg, catT, debug)

        w_inv = w_in.rearrange("(kc p) c -> p kc c", p=128)
        groups = [(0, 512), (512, 512), (1024, 512), (1536, 512), (2048, 512), (2560, 280)]

        GBUF = {0: 0, 1: 1, 2: 0, 3: 2, 4: 1, 5: 2}
        wst.append(A3[:, 0:2048].bitcast(BF16))
        sc.fence([("wst", 2)])

        def load_group(gi):
            c0, n = groups[gi]
            b = GBUF[gi]
            dst = wst[b].rearrange("p (kc c) -> p kc c", kc=8)[:, :, 0:n]
            sc.dma("pool", dst, w_inv[:, :, c0:c0 + n], (), (("wst", b),), "wst%d" % b)

        load_group(0)
        load_group(1)
        load_group(3)
        sc.op("pool", lambda e: e.memset(SIG[:, :], 0.0), writes=tuple(("SIG", tg) for tg in range(4)))
        sc.op("pool", lambda e: e.memset(vtok[:, :], 1.0), writes=tuple(("vtok", w_, tg) for w_ in range(2) for tg in range(4)))
        sc.op("pool", lambda e: e.memset(vctok[:, :], 1.0), writes=("vctok",))
        sc.dma("sp", Kc[64:101, :], krc_d, (), ("Kcr",), "kcr")

        sc.op("act", lambda e: e.activation(small[:, 0:4], vt[:, V_LAM:V_LAM + 4], AF.Exp, scale=-1.0),
              reads=("vec",), writes=("cc",))
        sc.op("act", lambda e: e.activation(small[:, 0:4], small[:, 0:4], AF.Ln, bias=one_ap, scale=1.0),
              reads=("cc", "cf"), writes=("cc",))
        sc.op("dve", lambda e: e.tensor_scalar(small[:, 0:4], small[:, 0:4], -8.0, None, ALU.mult),
              reads=("cc",), writes=("cc",))

        XC = A1[:, 10240:11264].bitcast(BF16)
        sc.fence([("XCb", tg) for tg in range(4)])

        jobs = []

        def add_job(gi, off, ncols, tg, post, after=None, staged=False):
            jobs.append(dict(gi=gi, off=off, ncols=ncols, tg=tg, post=post, after=after, staged=staged))

        def post_x(c):
            def f(tg, P, pk):
                cw = lambda k: vt[:, V_CW + c * 4 + k:V_CW + c * 4 + k + 1]
                lo, hi = tg * 512, (tg + 1) * 512

                def st0():
                    sc.op("dve", lambda e: e.tensor_copy(B[0][:, lo:hi], P[:, :]), reads=(pk,), writes=(("B", 0, tg),))
                    sc.op("dve", lambda e: e.tensor_scalar(B[1][:, lo:hi], B[0][:, lo:hi], cw(3), vt[:, V_CB + c:V_CB + c + 1], ALU.mult, ALU.add),
                          reads=(("B", 0, tg), "vec"), writes=(("B", 1, tg),))
                    prev = ((("B", 0, tg - 1),) if tg > 0 else ())
                    for sh, k in ((1, 2), (2, 1), (3, 0)):
                        l2 = max(lo, sh)
                        sc.op("dve", lambda e, sh=sh, k=k, l2=l2: e.scalar_tensor_tensor(
                            B[1][:, l2:hi], B[0][:, l2 - sh:hi - sh], cw(k), B[1][:, l2:hi], ALU.mult, ALU.add),
                            reads=(("B", 0, tg), ("B", 1, tg), "vec") + prev, writes=(("B", 1, tg),))

                def st1():
                    sc.op("act", lambda e: e.activation(XC[:, lo:hi], B[1][:, lo:hi], AF.Copy), reads=(("B", 1, tg),), writes=(("XCb", tg),))
                    for wt, pidx in ((wab, 4), (wib, 5)):
                        sc.op("pe", lambda e, wt=wt, pidx=pidx: e.matmul(PS[pidx][:, :], wt[:, c * 128:(c + 1) * 128], XC[:, lo:hi],
                                                                         start=True, stop=True),
                              reads=(("XCb", tg), "wab", "wib"), writes=(("ps", pidx),))
                    sc.op("act", lambda e: e.activation(B[2][:, lo:hi], PS[4][:, :], AF.Sigmoid, bias=vt[:, V_BA + c:V_BA + c + 1], scale=1.0),
                          reads=(("ps", 4), "vec"), writes=(("B", 2, tg),))
                    sc.op("act", lambda e: e.activation(B[3][:, lo:hi], PS[5][:, :], AF.Sigmoid, bias=vt[:, V_BI + c:V_BI + c + 1], scale=1.0),
                          reads=(("ps", 5), "vec"), writes=(("B", 3, tg),))
                    sc.op("pool", lambda e: e.tensor_tensor(B[3][:, lo:hi], B[3][:, lo:hi], B[1][:, lo:hi], ALU.mult),
                          reads=(("B", 3, tg), ("B", 1, tg)), writes=(("B", 3, tg),))

                def st2():
                    sc.op("act", lambda e: e.activation(B[2][:, lo:hi], B[2][:, lo:hi], AF.Exp, scale=small[:, c:c + 1]),
                          reads=(("B", 2, tg), "cc"), writes=(("B", 2, tg),))
                    sc.op("dve", lambda e: e.tensor_tensor(B[4][:, lo:hi], B[2][:, lo:hi], B[2][:, lo:hi], ALU.mult),
                          reads=(("B", 2, tg),), writes=(("B", 4, tg),))

                def st3():
                    sc.op("act", lambda e: e.activation(B[4][:, lo:hi], B[4][:, lo:hi], AF.Ln, bias=one_ap, scale=-1.0),
                          reads=(("B", 4, tg), "cf"), writes=(("B", 4, tg),))
                    sc.op("act", lambda e: e.activation(B[4][:, lo:hi], B[4][:, lo:hi], AF.Exp, scale=0.5),
                          reads=(("B", 4, tg),), writes=(("B", 4, tg),))
                    sc.op("pool", lambda e: e.tensor_tensor(B[3][:, lo:hi], B[3][:, lo:hi], B[4][:, lo:hi], ALU.mult),
                          reads=(("B", 3, tg), ("B", 4, tg)), writes=(("B", 3, tg),))

                def st4():
                    if tg > 0:
                        sc.op("dve", lambda e: e.scalar_tensor_tensor(B[3][:, lo:lo + 1], B[2][:, lo:lo + 1], B[4][:, lo - 1:lo],
                                                                      B[3][:, lo:lo + 1], ALU.mult, ALU.add),
                              reads=(("B", 2, tg), ("B", 3, tg), ("B", 4, tg - 1)), writes=(("B", 3, tg),))
                    sc.op("dve", lambda e: e.tensor_tensor_scan(B[4][:, lo:hi], B[2][:, lo:hi], B[3][:, lo:hi], 0.0, ALU.mult, ALU.add),
                          reads=(("B", 2, tg), ("B", 3, tg)), writes=(("B", 4, tg),))
                return [st0, None, st1, None, st2, None, st3, None, st4]
            return f

        def post_z(c):
            def f(tg, P, pk):
                j = tg % 2
                lo, hi = tg * 512, (tg + 1) * 512
                zt = T0[:, j * 512:(j + 1) * 512] if tg < 2 else A1[:, 11264 + j * 512:11264 + (j + 1) * 512]
                zk = ("T0", j) if tg < 2 else ("ZS", j)

                def st0():
                    sc.op("act", lambda e: e.activation(zt, P[:, :], AF.Sigmoid), reads=(pk,), writes=(zk,))
                    sc.op("dve", lambda e: e.tensor_tensor(zt, zt, P[:, :], ALU.mult), reads=(pk, zk), writes=(zk,))

                def st2():
                    sc.op("pool", lambda e: e.tensor_tensor(catT[:, c * S + lo:c * S + hi], B[4][:, lo:hi], zt, ALU.mult),
                          reads=(("B", 4, tg), zk), writes=(("cat", c),))
                return [st0, None, None, None, None, None, st2]
            return f

        def setup_qk():
            qkeys = ([("Qd", h, tg) for h in range(8) for tg in range(4)] + [("Qs", h, c) for h in range(8) for c in range(4)]
                     + [("Qr", h) for h in range(8)] + [("Kd", i, tg) for i in range(4) for tg in range(4)]
                     + [("Kr", i) for i in range(4)])
            sc.fence(qkeys)
            for h in range(8):
                sc.op("pool", lambda e, h=h: e.memset(Qh[h][64:96, :], 0.0), writes=tuple(("Qs", h, c) for c in range(4)))
                sc.dma("sp", Qh[h][96:101, :], qrows_d[h], (), (("Qr", h),), "qr")
            for i in range(4):
                sc.dma("sp", Kt[i][64:101, :], krs_d if i < 2 else krw_d, (), (("Kr", i),), "qr")
            sc.finalize_group("qr")

        def post_norm(gcol, outs_fn):
            def f(tg, P, pk):
                j = tg % 2
                fs = slice(j * 512, (j + 1) * 512)
                t1, t2 = T1[:, fs], T2[:, fs]

                def st0():
                    sc.op("act", lambda e: e.activation(t1, P[:, :], AF.Square), reads=(pk,), writes=(("T1", j),))
                    sc.op("pe", lambda e: e.matmul(PS[4 + j][:, :], blockones, t1, start=True, stop=True),
                          reads=(("T1", j), "cb"), writes=(("ps", 4 + j),))

                def st1():
                    sc.op("act", lambda e: e.activation(t2, PS[4 + j][:, :], AF.Ln, bias=eps_ap, scale=1.0 / 64),
                          reads=(("ps", 4 + j), "cf"), writes=(("T2", j),))
                    sc.op("act", lambda e: e.activation(t2, t2, AF.Exp, scale=-0.5), reads=(("T2", j),), writes=(("T2", j),))
                    for half, (dst, key) in enumerate(outs_fn(tg)):
                        rows = slice(half * 64, (half + 1) * 64)
                        sc.op("dve", lambda e, dst=dst, rows=rows: e.scalar_tensor_tensor(
                            dst, P[rows, :], vt[rows, gcol:gcol + 1], t2[rows, :], ALU.mult, ALU.mult),
                            reads=(pk, ("T2", j), "vec"), writes=(key,))
                return [st0, st1]
            return f

        PS6b = PS[6][:, 0:256].bitcast(BF16)

        def post_v(which):
            def f(tg, P, pk):
                jv = tg % 2
                vts = T1[:, jv * 512:(jv + 1) * 512]
                sc.op("dve", lambda e: e.tensor_copy(vts, P[:, :]), reads=(pk,), writes=(("T1", jv),))
                for q4 in range(4):
                    sc.op("pe", lambda e, q4=q4: e.transpose(PS6b[:, q4 * 128:(q4 + 1) * 128], vts[:, q4 * 128:(q4 + 1) * 128], ident),
                          reads=(("T1", jv), "cb"), writes=(("ps", 6),))
                base = which * 4096 + tg * 1024
                dst = vtok[:, base:base + 1024].rearrange("p (k g x) -> p k g x", k=4, g=2)[:, :, :, 0:64]
                src = PS6b[:, 0:512].rearrange("p (k g d) -> p k g d", k=4, g=2)
                sc.op("act", lambda e: e.activation(dst, src, AF.Copy), reads=(("ps", 6),), writes=(("vtok", which, tg),))
            return f

        def k_outs(i0):
            return lambda tg: [(Kt[i0][0:64, tgs(tg)], ("Kd", i0, tg)), (Kt[i0 + 1][0:64, tgs(tg)], ("Kd", i0 + 1, tg))]


        def compression():
            w1v_ = [w1kb.rearrange("p (c h) -> p c h", c=16), w1vb.rearrange("p (c h) -> p c h", c=16)]
            w2v_ = [w2kb.rearrange("p (a d) -> p a d", a=2), w2vb.rearrange("p (a d) -> p a d", a=2)]
            for kv in range(2):
                for half in range(2):
                    col = 400 + kv * 2 + half
                    for c in range(16):
                        sc.op("pe", lambda e, c=c, col=col, kv=kv, half=half: e.matmul(
                            PS[7][:, col:col + 1], w1v_[kv][:, c, half * 128:(half + 1) * 128], posb[:, c:c + 1],
                            start=(c == 0), stop=(c == 15)),
                            reads=("w1kb", "w1vb", "posb"), writes=(("ps", 7),))
            sc.op("dve", lambda e: e.tensor_copy(small[:, 4:8], PS[7][:, 400:404]), reads=(("ps", 7),), writes=("bpos",))
            yield
            for kv in range(2):
                for g in range(2):
                    rows = slice(g * 64, (g + 1) * 64)
                    src = KCB[kv][rows, :].rearrange("p (n s) -> p s n", s=16)
                    xc = KB2[g][:, 0:2032].rearrange("p (c n) -> p c n", c=16)
                    for lpar in range(2):
                        prow = slice(lpar * 64, (lpar + 1) * 64)
                        for hi_, eng in ((0, "dve"), (1, "dve")):
                            sc.op(eng, lambda e, lpar=lpar, prow=prow, hi_=hi_: e.tensor_copy(
                                xc[prow, hi_ * 8:(hi_ + 1) * 8, :], src[:, lpar::2, hi_:hi_ + 127]),
                                reads=(("KCB", kv),), writes=(("KB2", g),))
                    yield
                for g in range(2):
                    for half in range(2):
                        o0 = (half * 2 + g) * 127
                        for c in range(16):
                            sc.op("pe", lambda e, c=c, g=g, o0=o0, half=half: e.matmul(
                                PS[6][:, o0:o0 + 127], w1v_[kv][:, c, half * 128:(half + 1) * 128],
                                KB2[g][:, c * 127:(c + 1) * 127], start=(c == 0), stop=(c == 15)),
                                reads=(("KB2", g), "w1kb", "w1vb"), writes=(("ps", 6),))
                for half in range(2):
                    sc.op("act", lambda e, half=half: e.activation(
                        hidT[:, kv * 508 + half * 254:kv * 508 + (half + 1) * 254], PS[6][:, half * 254:(half + 1) * 254],
                        AF.Silu, bias=small[:, 4 + kv * 2 + half:5 + kv * 2 + half], scale=1.0),
                        reads=(("ps", 6), "bpos"), writes=(("hidT", kv),))
                if kv == 0:
                    for g in range(2):
                        for half in range(2):
                            sc.op("pe", lambda e, g=g, half=half: e.matmul(
                                PS[7][0:64, g * 127:(g + 1) * 127], w2v_[0][:, half, :],
                                hidT[:, half * 254 + g * 127:half * 254 + (g + 1) * 127], start=(half == 0), stop=(half == 1)),
                                reads=(("hidT", 0), "w2kb"), writes=(("ps", 7),))
                    sc.op("act", lambda e: e.activation(T1[0:64, 0:254], PS[7][0:64, 0:254], AF.Square),
                          reads=(("ps", 7),), writes=(("T1", 0),))
                    sc.op("pe", lambda e: e.matmul(PS[4][0:64, 0:254], blockones[0:64, 0:64], T1[0:64, 0:254], start=True, stop=True),
                          reads=(("T1", 0), "cb"), writes=(("ps", 4),))
                    sc.op("act", lambda e: e.activation(T2[0:64, 0:254], PS[4][0:64, 0:254], AF.Ln, bias=eps_ap[0:64, :], scale=1.0 / 64),
                          reads=(("ps", 4), "cf"), writes=(("T2", 0),))
                    sc.op("act", lambda e: e.activation(T2[0:64, 0:254], T2[0:64, 0:254], AF.Exp, scale=-0.5),
                          reads=(("T2", 0),), writes=(("T2", 0),))
                    sc.op("dve", lambda e: e.scalar_tensor_tensor(Kc[0:64, 0:254], PS[7][0:64, 0:254], vt[0:64, V_GKC:V_GKC + 1],
                                                                  T2[0:64, 0:254], ALU.mult, ALU.mult),
                          reads=(("ps", 7), ("T2", 0), "vec"), writes=("Kcd",))
                else:
                    for g in range(2):
                        for half in range(2):
                            sc.op("pe", lambda e, g=g, half=half: e.matmul(
                                PS[7][0:127, 256 + g * 64:256 + (g + 1) * 64],
                                hidT[:, 508 + half * 254 + g * 127:508 + half * 254 + (g + 1) * 127], w2v_[1][:, half, :],
                                start=(half == 0), stop=(half == 1)),
                                reads=(("hidT", 1), "w2vb"), writes=(("ps", 7),))
                    sc.op("act", lambda e: e.activation(vctok[0:127, :].rearrange("p (g x) -> p g x", g=2)[:, :, 0:64],
                                                        PS[7][0:127, 256:384].rearrange("p (g d) -> p g d", g=2), AF.Copy),
                          reads=(("ps", 7),), writes=("vctok",))
            sc.fence([("sz", p_, tg) for p_ in range(4) for tg in range(4)])

        extras = []
        for kv in range(2):
            for tg in range(4):
                def fkc(tg, P, pk, kv=kv):
                    sc.op("dve", lambda e: e.tensor_copy(KCB[kv][:, tgs(tg)], P[:, :]), reads=(pk,), writes=(("KCB", kv),))
                extras.append((3, kv * 128, 128, tg, fkc))
        for tg in range(4):
            extras.append((3, 384, 128, tg, post_v(0)))
        cgen = compression()
        ei = 0
        csteps = 0
        for c in range(4):
            for slot in (("x", 0), ("x", 1), ("z", 0), "E", ("x", 2), ("z", 1), "E", ("x", 3), ("z", 2), ("z", 3), "E"):
                if slot == "E":
                    add_job(*extras[ei])
                    ei += 1
                else:
                    kind, tg = slot
                    last = (c == 3 and tg == 3)
                    if kind == "x":
                        add_job(0, c * 128, 128, tg, post_x(c), after=((lambda: load_group(2)) if last else None), staged=True)
                    else:
                        add_job(1, c * 128, 128, tg, post_z(c), after=((lambda: load_group(4)) if last else None), staged=True)
                        if last:
                            def lru_done():
                                for _ in cgen:
                                    pass
                                setup_qk()
                            jobs[-1]["after_post"] = lru_done
                if ei > 8 and csteps < 7 and not jobs[-1].get("after_post"):
                    jobs[-1]["after_post"] = (lambda: next(cgen, None))
                    csteps += 1
        for tg in range(4):
            add_job(4, 128, 128, tg, post_v(1))
        for _ in range(1):
            add_job(2, 0, 0, 0, (lambda tg, P, pk: None))
        for tg in range(4):
            add_job(3, 256, 128, tg, post_norm(V_GKS, k_outs(0)), after=((lambda: load_group(5)) if tg == 3 else None), staged=True)
        for p_ in range(4):
            for tg in range(4):
                outs = (lambda p_: (lambda tg: [(Qh[2 * p_][0:64, tgs(tg)], ("Qd", 2 * p_, tg)),
                                                (Qh[2 * p_ + 1][0:64, tgs(tg)], ("Qd", 2 * p_ + 1, tg))]))(p_)
                add_job(2, p_ * 128, 128, tg, post_norm(V_GQ, outs), staged=True)
        for tg in range(4):
            add_job(4, 0, 128, tg, post_norm(V_GKW, k_outs(2)), staged=True)

        for p_ in range(4):
            gi, off = (4, 256 + p_ * 128) if p_ < 2 else (5, (p_ - 2) * 128)
            for tg in range(4):
                def f(tg, P, pk, p_=p_):
                    sc.op("act", lambda e: e.activation(siluz[:, p_ * S + tg * 512:p_ * S + (tg + 1) * 512], P[:, :], AF.Silu),
                          reads=(pk,), writes=(("sz", p_, tg),))
                add_job(gi, off, 128, tg, f)

        for tg in range(4):
            def f(tg, P, pk):
                j = tg % 2
                t2 = T2[0:24, j * 512:(j + 1) * 512]
                sc.op("act", lambda e: e.activation(t2, P[0:24, :], AF.Sigmoid), reads=(pk,), writes=(("T2", j),))
                sc.op("act", lambda e: e.activation(SIG[0:24, tgs(tg)], t2, AF.Copy), reads=(("T2", j),), writes=(("SIG", tg),))
                sc.op("dve", lambda e: e.tensor_tensor(SIG[32:56, tgs(tg)], t2, SIG[0:24, tgs(tg)], ALU.subtract),
                      reads=(("T2", j), ("SIG", tg)), writes=(("SIG", tg),))
            add_job(5, 256, 24, tg, f)

        def emit_proj(ji):
            jb = jobs[ji]
            if jb.get("pre"):
                jb["pre"]()
            b = GBUF[jb["gi"]]
            wv = wst[b].rearrange("p (kc c) -> p kc c", kc=8)
            pi = ji % 4
            tg, off, ncols = jb["tg"], jb["off"], jb["ncols"]
            for kc in range(8 if ncols else 0):
                sc.op("pe", lambda e, kc=kc: e.matmul(PS[pi][0:ncols, :], wv[:, kc, off:off + ncols], xb[kc][:, tg * 512:(tg + 1) * 512],
                                                      start=(kc == 0), stop=(kc == 7)),
                      reads=(("wst", b), ("xb", kc)), writes=(("ps", pi),))
            if ji >= 36 and PDUM:
                sc.op("pe", lambda e: e.matmul(PS[7][:, 0:PDUM], ident, mc[:, 0:PDUM], start=True, stop=True),
                      reads=("cb", "mc"), writes=(("ps", 7),))
            if jb.get("after"):
                jb["after"]()

        LOOK = 2
        MAXS = 9
        PAIR_END = 50
        assert jobs[49]["gi"] == 3 and jobs[48]["ncols"] == 0

        def prep_job(t):
            jb_ = jobs[t]
            if jb_.get("staged"):
                jb_["stages"] = jb_["post"](jb_["tg"], PS[t % 4], ("ps", t % 4))
            else:
                jb_["stages"] = [(lambda jb_=jb_, t=t: jb_["post"](jb_["tg"], PS[t % 4], ("ps", t % 4)))]

        for ji in range(min(LOOK, len(jobs))):
            emit_proj(ji)
        t = 0
        while t < len(jobs) + MAXS:
            ts_ = (t, t + 1) if t < PAIR_END else (t,)
            for t_ in ts_:
                if t_ + LOOK < len(jobs):
                    emit_proj(t_ + LOOK)
            for t_ in ts_:
                if t_ < len(jobs):
                    prep_job(t_)
            done = []
            for s_ in reversed(range(MAXS)):
                for t_ in ts_:
                    j_ = t_ - s_
                    if 0 <= j_ < len(jobs):
                        stg = jobs[j_]["stages"]
                        if s_ < len(stg) and stg[s_] is not None:
                            stg[s_]()
                        if s_ == max(len(stg) - 1, 0):
                            done.append(j_)
            for j_ in sorted(done):
                if jobs[j_].get("after_post"):
                    jobs[j_]["after_post"]()
            t += len(ts_)

        if stop_after == "lru":
            dbg["lru_out"] = (catT[:, 0:4 * S], ("cat", 3), [128, 4 * S], BF16)
            return finish(nc, sc, es, yT, dbg, catT, debug)
        if stop_after == "proj":
            dbg["Q0"] = (Qh[0], ("Qd", 0, 3), [128, S], BF16)
            dbg["Q5"] = (Qh[5], ("Qd", 5, 3), [128, S], BF16)
            dbg["Ks1"] = (Kt[1], ("Kd", 1, 3), [128, S], BF16)
            dbg["Kw0"] = (Kt[2], ("Kd", 2, 3), [128, S], BF16)
            dbg["Kc"] = (Kc[:, :], "Kcd", [128, 254], BF16)
            dbg["vctok"] = (vctok[:, :], "vctok", [128, 256], BF16)
            dbg["vtok"] = (vtok[:, :], ("vtok", 1, 3), [128, 8192], BF16)
            dbg["siluz"] = (siluz[:, :], ("sz", 3, 3), [128, 4 * S], BF16)
            dbg["SIG"] = (SIG[:, :], ("SIG", 3), [128, S], BF16)
            return finish(nc, sc, es, yT, dbg, catT, debug)

        w_outb = A2[:, 0:8192].rearrange("p (kc c) -> p kc c", kc=8)
        PT = [A2[:, 8192 + r * 512:8192 + (r + 1) * 512] for r in range(3)]
        Rt = A2[:, 9728:10752].bitcast(F32)
        Wt = A2[:, 10752:11776].bitcast(F32)
        Tm = A2[:, 11776:12800].bitcast(F32)
        TS = [(Rt, Wt, Tm), (A2[:, 12800:13824].bitcast(F32), A2[:, 13824:14848].bitcast(F32), A2[:, 14848:15872].bitcast(F32))]
        acc = [A3[:, h * 512:(h + 1) * 512] for h in range(8)]
        sc.fence(["w_outb"] + [("PT", r) for r in range(3)] + [("Rt", j) for j in range(2)] + [("Wt", j) for j in range(2)]
                 + [("Tm", j) for j in range(2)] + [("acc", h) for h in range(8)]
                 + [("cat", 4 + p_, c) for p_ in range(4) for c in range(4)]
                 + [("ors", o_, hf_) for o_ in range(3) for hf_ in range(2)] + ["psG", "psI"])
        sc.dma("pool", w_outb, w_out.rearrange("(kc p) c -> p kc c", p=128), (), ("w_outb",), "w_outb")

        for i_ in range(WARM):
            sc.op("pe", lambda e: e.matmul(PS[0][:, :], ident, mc[:, 0:512], start=True, stop=True),
                  reads=("cb", "mc"), writes=(("ps", 0),))
        PS7b = PS[6][:, 256:320].bitcast(BF16)
        selh_v = selh[:, :].rearrange("p (k d) -> p k d", k=24)
        ectr = [0]
        PS6v = PS[6][:, 0:132].rearrange("p (t m) -> p t m", m=33)
        impacc_v = impacc[:, :].rearrange("p (t m) -> p t m", m=32)
        impt_v = impt[:, :].rearrange("p (t m) -> p t m", m=32)
        score_v = score[:, :].rearrange("p (t m) -> p t m", m=32)
        rI = small[:, 8:12]
        m8 = small[:, 16:48]
        gctr = [0]
        ORSB = [3, 4, 7]
        NORS = [2]

        def epilogue(h, c, br, o, last_br):
            cols = slice(c * 512, (c + 1) * 512)
            j = ectr[0] % 2
            ectr[0] += 1
            Rt_, Wt_, Tm_ = TS[j]
            P_ = slice(0, 64)
            R_ = slice(64, 128)
            ORS = PS[ORSB[o]]
            use_dve = br in DVE_RECIP and c >= DVE_RECIP_MINC and not (br == 0 and c == 0)
            if use_dve:
                Osrc, okeys = ORS[P_, :], (("ors", o, 0),)
            else:
                Osrc, okeys = T0[P_, j * 512:(j + 1) * 512], (("T0", j),)
                sc.op("dve", lambda e: e.tensor_copy(Osrc, ORS[P_, :]), reads=(("ors", o, 0),), writes=okeys)
            sc.op("pe", lambda e: e.matmul(PS[5][:, :], selh_v[:, 3 * h + (0, 2, 1)[br], :], SIG[:, cols], start=True, stop=True),
                  reads=(("SIG", c), "selh"), writes=("psG",))
            if br == 0 and c == 0:
                sc.op("act", lambda e: e.activation(Rt_[R_, :], ORS[R_, :], AF.Ln, bias=cf[R_, CF_CV + 2:CF_CV + 3], scale=1.0),
                      reads=(("ors", o, 1), "cf"), writes=(("Rt", j),))
            elif not (br in DVE_RECIP and c >= DVE_RECIP_MINC):
                sc.op("act", lambda e: e.activation(Rt_[R_, :], ORS[R_, :], AF.Ln), reads=(("ors", o, 1),), writes=(("Rt", j),))
            if br in DVE_RECIP and c >= DVE_RECIP_MINC and not (br == 0 and c == 0):
                sc.op("dve", lambda e: e.reciprocal(Rt_[R_, :], ORS[R_, :]), reads=(("ors", o, 1),), writes=(("Rt", j),))
            else:
                sc.op("act", lambda e: e.activation(Rt_[R_, :], Rt_[R_, :], AF.Exp, scale=-1.0), reads=(("Rt", j),), writes=(("Rt", j),))
            sc.op("dve", lambda e: e.tensor_tensor(Wt_[P_, :], Rt_[R_, :], PS[5][R_, :], ALU.mult),
                  reads=(("Rt", j), "psG"), writes=(("Wt", j),))
            if br not in branches:
                pass
            elif br == branches[0]:
                sc.op("dve", lambda e: e.tensor_tensor(acc[h][P_, :], Osrc, Wt_[P_, :], ALU.mult),
                      reads=okeys + (("Wt", j),), writes=(("acc", h),))
            else:
                sc.op("dve", lambda e: e.tensor_tensor(Tm_[P_, :], Osrc, Wt_[P_, :], ALU.mult),
                      reads=okeys + (("Wt", j),), writes=(("Tm", j),))
                sc.op("pool", lambda e: e.tensor_tensor(acc[h][P_, :], acc[h][P_, :], Tm_[P_, :], ALU.add),
                      reads=(("acc", h), ("Tm", j)), writes=(("acc", h),))
            if last_br:
                pr = h // 2
                cs = slice((4 + pr) * S + c * 512, (4 + pr) * S + (c + 1) * 512)
                zs = slice(pr * S + c * 512, pr * S + (c + 1) * 512)
                if h % 2 == 0:
                    sc.op("pool", lambda e: e.tensor_tensor(catT[P_, cs], acc[h][P_, :], siluz[P_, zs], ALU.mult),
                          reads=(("acc", h), ("sz", pr, c)), writes=(("cat", 4 + pr, c),))
                else:
                    sc.op("pool", lambda e: e.tensor_copy(acc[h][R_, :], acc[h][P_, :]), reads=(("acc", h),), writes=(("acc", h),))
                    sc.op("pool", lambda e: e.tensor_tensor(catT[R_, cs], acc[h][R_, :], siluz[R_, zs], ALU.mult),
                          reads=(("acc", h), ("sz", pr, c)), writes=(("cat", 4 + pr, c),))

        xre = [A3[:, 4096:4608], A3[:, 4608:5120]]
        ost = [A3[:, 5120:5632], A3[:, 5632:6144]]
        sc.fence([("ost", 0), ("ost", 1), ("xre", 0), ("xre", 1)])
        octr = [0]

        def outproj_slots(tc, banks, gap, nbuf=2):
            tcs = slice(tc * 512, (tc + 1) * 512)
            if nbuf > 2:
                for i_ in range(nbuf):
                    sc.dma("sp", xre[i_], xT[i_ * 128:(i_ + 1) * 128, tcs], (), (("xre", i_),), "xre%d" % i_)
            for cc in range(8):
                i2 = (octr[0] % 2) if nbuf == 2 else cc
                bk = banks[octr[0] % len(banks)]
                octr[0] += 1
                bkeys = (("ors", 2, 0), ("ors", 2, 1)) if bk == 7 else (("ps", bk),)
                for kc in range(8):
                    def mm(kc=kc, cc=cc, i2=i2, bk=bk):
                        if kc == 0 and nbuf == 2:
                            sc.dma("sp", xre[i2], xT[cc * 128:(cc + 1) * 128, tcs], (), (("xre", i2),), "xre%d" % i2)
                        rkeys = (("cat", kc),) if kc < 4 else (("cat", kc, tc),)
                        sc.op("pe", lambda e: e.matmul(PS[bk][:, :], w_outb[:, kc, cc * 128:(cc + 1) * 128],
                                                       catT[:, kc * S + tc * 512:kc * S + (tc + 1) * 512],
                                                       start=(kc == 0), stop=(kc == 7)),
                              reads=("w_outb",) + rkeys, writes=bkeys)
                    yield mm

                def ev(cc=cc, i2=i2, bk=bk):
                    sc.op("dve", lambda e: e.tensor_tensor(ost[i2], PS[bk][:, :], xre[i2], ALU.add),
                          reads=bkeys + (("xre", i2),), writes=(("ost", i2),))
                    oq = "sp" if (nbuf == 2 or cc % 2 == 0) else "act"
                    sc.dma(oq, yT[cc * 128:(cc + 1) * 128, tcs], ost[i2], (("ost", i2),), (), "ost%d" % i2)
                yield ev
                for _ in range(gap):
                    yield None

        for c in range(4):
            cols = slice(c * 512, (c + 1) * 512)
            NORS[0] = 3 if c < 2 else 2
            if c < 2:
                filler = iter(())
            elif c == 2:
                filler = outproj_slots(0, [7], 8)
            else:
                def two_():
                    for a_ in outproj_slots(1, [7], 3):
                        yield a_
                    for a_ in outproj_slots(2, [7], 3):
                        yield a_
                filler = two_()
            units = []

            def branch_units(kind, br, h):
                ul = []
                if kind == "w":
                    for b_ in range(4):
                        kt = 4 * c - 4 + b_
                        if kt >= 0:
                            ul.append(dict(kt=kt, qlo=0, qhi=128 * (b_ + 1), mask=("atri", 128 * b_)))
                else:
                    for kt in range(4 * c):
                        ul.append(dict(kt=kt, qlo=0, qhi=512, mask=None))
                for a_ in range(4):
                    ul.append(dict(kt=4 * c + a_, qlo=128 * a_, qhi=512, mask=("tri", 128 * a_)))
                merged = []
                cur = None
                for i_, u in enumerate(ul):
                    n_ = u["qhi"] - u["qlo"]
                    u["sfirst"] = (i_ == 0)
                    u["slast"] = (i_ == len(ul) - 1)
                    if cur is not None and cur["tot"] + n_ <= 512:
                        u["off"] = cur["tot"]
                        cur["subs"].append(u)
                        cur["tot"] += n_
                    else:
                        u["off"] = 0
                        cur = dict(subs=[u], tot=n_)
                        merged.append(cur)
                for i_, m in enumerate(merged):
                    m.update(kind=kind, h=h, br=br, first=(i_ == 0), last=(i_ == len(merged) - 1))
                return merged

            for h in range(8):
                units.append(dict(kind="c", h=h, first=True, last=True, br=0))
                units += branch_units("w", 1, h)
            for h in range(8):
                units += branch_units("s", 2, h)

            def emit_S(u, r):
                h = u["h"]
                g = h // 4
                if u["kind"] == "c":
                    sc.op("pe", lambda e: e.matmul(PS[r][0:127, :], Kc[0:101, g * 127:(g + 1) * 127], Qh[h][0:101, cols],
                                                   start=True, stop=False),
                          reads=("Kcd", "Kcr", ("Qd", h, c), ("Qs", h, c), ("Qr", h)), writes=(("ps", r),))
                    sc.op("pe", lambda e: e.matmul(PS[r][0:127, :], ident[0:127, 0:127], mc[0:127, cols], start=False, stop=True),
                          reads=("cb", "mc"), writes=(("ps", r),))
                    return
                ki = g if u["kind"] == "s" else 2 + g
                for sb_ in u["subs"]:
                    kt, qlo, qhi, off = sb_["kt"], sb_["qlo"], sb_["qhi"], sb_["off"]
                    n_ = qhi - qlo
                    sc.op("pe", lambda e: e.matmul(PS[r][:, off:off + n_], Kt[ki][0:101, kt * 128:(kt + 1) * 128],
                                                   Qh[h][0:101, c * 512 + qlo:c * 512 + qhi], start=True, stop=(sb_["mask"] is None)),
                          reads=(("Kd", ki, kt // 4), ("Kr", ki), ("Qd", h, c), ("Qs", h, c), ("Qr", h)), writes=(("ps", r),))
                    if sb_["mask"] is not None:
                        mt, mlo = sb_["mask"]
                        msk = TRI if mt == "tri" else ATRI
                        mo = off + (mlo - qlo)
                        sc.op("pe", lambda e: e.matmul(PS[r][:, mo:mo + 128], ident, msk, start=False, stop=True),
                              reads=("cb",), writes=(("ps", r),))

            def emit_exp(u, r):
                h = u["h"]
                if u["kind"] == "c":
                    sc.op("act", lambda e: e.activation(PT[r][0:127, :], PS[r][0:127, :], AF.Exp, scale=0.125),
                          reads=(("ps", r),), writes=(("PT", r),))
                    return
                tot = u["tot"]
                sc.op("act", lambda e: e.activation(PT[r][:, 0:tot], PS[r][:, 0:tot], AF.Exp, scale=0.125),
                      reads=(("ps", r),), writes=(("PT", r),))

            def emit_PV(u, r):
                h = u["h"]
                g = h // 4
                pb = (h % 2) * 64
                rb = 64 - pb
                if u["first"]:
                    gctr[0] += 1
                u["o"] = gctr[0] % NORS[0]
                o = u["o"]
                ORS = PS[ORSB[o]]
                if u["kind"] == "c":
                    sc.op("pe", lambda e: e.matmul(ORS[:, :], vctok[0:127, g * 128:(g + 1) * 128], PT[r][0:127, :],
                                                   start=True, stop=True),
                          reads=(("PT", r), "vctok"), writes=(("ors", o, 0), ("ors", o, 1)))
                    for tt in range(4):
                        sc.op("pe", lambda e, tt=tt: e.matmul(PS[6][:, tt * 33:(tt + 1) * 33], PT[r][0:127, tt * 128:(tt + 1) * 128],
                                                              ova[0:127, 0:33], start=True, stop=True),
                              reads=(("PT", r), "ova"), writes=("psI",))
                    jh = h % 4
                    sc.op("dve", lambda e: e.tensor_scalar(rI, PS6v[:, :, 32], 1e-30, None, ALU.max),
                          reads=("psI",), writes=("rI",))
                    sc.op("dve", lambda e: e.reciprocal(rI, rI), reads=("rI",), writes=("rI",))
                    rIb = rI.unsqueeze(2).to_broadcast([128, 4, 32])
                    if jh == 0:
                        sc.op("dve", lambda e: e.tensor_tensor(impacc_v, PS6v[:, :, 0:32], rIb, ALU.mult),
                              reads=("psI", "rI"), writes=("impacc",))
                    else:
                        sc.op("dve", lambda e: e.tensor_tensor(impt_v, PS6v[:, :, 0:32], rIb, ALU.mult),
                              reads=("psI", "rI"), writes=("impt",))
                        sc.op("dve", lambda e: e.tensor_tensor(impacc[:, :], impacc[:, :], impt[:, :], ALU.add),
                              reads=("impacc", "impt"), writes=("impacc",))
                    if jh == 3:
                        a0 = CF_A + c * 128
                        b0 = CF_B + c * 128
                        sc.op("dve", lambda e: e.tensor_tensor(score[:, :], impacc[:, :], cf[:, a0:a0 + 128], ALU.mult),
                              reads=("impacc", "cf"), writes=("score",))
                        sc.op("dve", lambda e: e.tensor_tensor(score[:, :], score[:, :], cf[:, b0:b0 + 128], ALU.add),
                              reads=("score", "cf"), writes=("score",))
                        for tt in range(4):
                            sc.op("dve", lambda e, tt=tt: e.max(m8[:, tt * 8:(tt + 1) * 8], score[:, tt * 32:(tt + 1) * 32]),
                                  reads=("score",), writes=("m8",))
                        for tt in range(4):
                            sc.op("dve", lambda e, tt=tt: e.tensor_scalar(impt[:, tt * 32:(tt + 1) * 32], score[:, tt * 32:(tt + 1) * 32],
                                                                          m8[:, tt * 8 + 7:tt * 8 + 8], 1.0, ALU.is_ge, ALU.subtract),
                                  reads=("score", "m8"), writes=("impt",))
                        sg_ = selt[:, g * 128:(g + 1) * 128]
                        sc.op("dve", lambda e: e.tensor_scalar(sg_, impt[:, :], -NEGM, None, ALU.mult),
                              reads=("impt",), writes=(("selt", g),))

                        def sel_pe(g=g, sg_=sg_):
                            sc.op("pe", lambda e: e.transpose(PS7b[:, 0:128], sg_, ident),
                                  reads=(("selt", g), "cb"), writes=("psI",))
                            for tt in range(4):
                                sc.op("dve", lambda e, tt=tt: e.tensor_copy(Qh[4 * g][64:96, c * 512 + tt * 128:c * 512 + (tt + 1) * 128],
                                                                            PS7b[tt * 32:(tt + 1) * 32, 0:128]),
                                      reads=("psI",), writes=(("Qs", 4 * g, c),))
                            for k_ in range(1, 4):
                                sc.op("pool", lambda e, k_=k_: e.tensor_copy(Qh[4 * g + k_][64:96, cols], Qh[4 * g][64:96, cols]),
                                      reads=(("Qs", 4 * g, c),), writes=(("Qs", 4 * g + k_, c),))
                        deferred.append((u["idx"] + 6, sel_pe))
                    return
                which = 0 if u["kind"] == "s" else 1
                for sb_ in u["subs"]:
                    kt, qlo, qhi, off = sb_["kt"], sb_["qlo"], sb_["qhi"], sb_["off"]
                    n_ = qhi - qlo
                    vb = ((which * 16 + kt) * 2 + g) * 128
                    sc.op("pe", lambda e: e.matmul(ORS[:, qlo:qhi], vtok[:, vb:vb + 128], PT[r][:, off:off + n_],
                                                   start=sb_["sfirst"], stop=sb_["slast"], skip_group_check=True),
                          reads=(("PT", r), ("vtok", which, kt // 4)), writes=(("ors", o, 0), ("ors", o, 1)))

            emit_S(units[0], 0)
            emit_S(units[1], 1)
            pending = []
            deferred = []
            for i_, u in enumerate(units):
                u["idx"] = i_
                r = i_ % 3
                if i_ + 2 < len(units):
                    emit_S(units[i_ + 2], (i_ + 2) % 3)
                emit_exp(u, r)
                keep = []
                for ep in pending:
                    if ep[0] <= i_:
                        epilogue(*ep[1])
                    else:
                        keep.append(ep)
                pending = keep
                for d_ in [d_ for d_ in deferred if d_[0] <= i_]:
                    d_[1]()
                    deferred.remove(d_)
                emit_PV(u, r)
                act_ = next(filler, "done")
                if act_ == "done" or c < 2:
                    if NDUM and c >= 2:
                        sc.op("pe", lambda e: e.matmul(PS[7][:, 0:NDUM], ident, mc[:, 0:NDUM], start=True, stop=True),
                              reads=("cb", "mc"), writes=(("ors", 2, 0), ("ors", 2, 1)))
                elif act_ is not None:
                    act_()
                if u["last"]:
                    j_ = i_
                    while not units[j_]["first"]:
                        j_ -= 1
                    nxt_long = (i_ + 1 < len(units)) and not (units[i_ + 1]["kind"] == "c")
                    nxt_long = (i_ + 1 < len(units)) and not (units[i_ + 1]["kind"] == "c")
                    pending.append((i_ + (2 if nxt_long else 1), (u["h"], c, u["br"], units[j_]["o"], u["br"] == 2)))
            for ep in pending:
                epilogue(*ep[1])
            for d_ in deferred:
                d_[1]()
            for act_ in filler:
                if act_:
                    act_()

        if stop_after == "attn":
            dbg["cat"] = (catT[:, :], ("cat", 7, 3), [128, 8 * S], BF16)
            return finish(nc, sc, es, yT, dbg, catT, debug)

        xre = [A1[:, i_ * 512:(i_ + 1) * 512] for i_ in range(8)]
        ost = [A1[:, 4096 + i_ * 512:4096 + (i_ + 1) * 512] for i_ in range(8)]
        sc.fence([("xre", i_) for i_ in range(8)] + [("ost", i_) for i_ in range(8)])
        for act_ in outproj_slots(3, [0, 1, 2], 0, nbuf=8):
            if act_:
                act_()
        sc.wait_all_dma("sp", ["ost%d" % i_ for i_ in range(8)])
    return nc


def finish(nc, sc, es, yT, dbg, catT, debug):
    for name, (ap, key, shape, dt) in dbg.items():
        d = nc.dram_tensor("dbg_" + name, list(shape), dt, kind="ExternalOutput").ap()
        sc.dma("sp", d, ap, (key,), (), "dbg_" + name)
    sc.wait_all_dma("sp", [k for k in sc.dsems if k.startswith("dbg_")])
    return nc


def prep_shared(inp):
    f = np.float32
    m = {}
    m["w_in"] = np.ascontiguousarray(inp["w_in"][0], dtype=f)
    m["w_out"] = np.ascontiguousarray(inp["w_out"][0], dtype=f)
    vec = np.zeros((128, V_W), f)
    vec[:, V_G:V_G + 8] = inp["norm_g"][0].reshape(8, 128).T
    cw = inp["conv_w"][0]
    vec[:, V_CW:V_CW + 16] = cw.reshape(4, 4, 128).transpose(2, 1, 0).reshape(128, 16)
    for col, name in ((V_CB, "conv_b"), (V_BA, "lru_ba"), (V_BI, "lru_bi"), (V_LAM, "lru_lambda")):
        vec[:, col:col + 4] = inp[name][0].reshape(4, 128).T
    for col, name in ((V_GQ, "q_norm_g"), (V_GKS, "ks_norm_g"), (V_GKW, "kw_norm_g"), (V_GKC, "kc_norm_g")):
        vec[:, col] = np.tile(inp[name][0], 2)
    m["vec"] = vec
    for name, key in (("wa", "lru_wa"), ("wi", "lru_wi")):
        w = inp[key][0]
        bd = np.zeros((128, 4, 128), f)
        for n_ in range(8):
            bd[(n_ % 2) * 64:(n_ % 2) * 64 + 64, n_ // 2, (n_ % 2) * 64:(n_ % 2) * 64 + 64] = w[n_]
        m[name] = np.ascontiguousarray(bd.reshape(128, 512))
    pos = inp["cmp_pos"][0]
    m["posT2"] = np.ascontiguousarray(pos.reshape(16, 2, 64).transpose(1, 2, 0).reshape(128, 16), dtype=f)
    for name, key in (("w1k", "cmp_k_w1"), ("w1v", "cmp_v_w1")):
        w = inp[key][0]
        m[name] = np.ascontiguousarray(w.reshape(16, 2, 64, 256).transpose(1, 2, 0, 3).reshape(128, 4096), dtype=f)
    for name, key in (("w2k", "cmp_k_w2"), ("w2v", "cmp_v_w2")):
        w = inp[key][0]
        m[name] = np.ascontiguousarray(w.reshape(2, 128, 64).transpose(1, 0, 2).reshape(128, 128), dtype=f)
    m.update(host_consts())
    return m


def kernel(**inputs):
    inp = {k: np.asarray(v) for k, v in inputs.items()}
    shared = prep_shared(inp)
    nc = build()
    in_maps = []
    for b in range(8):
        mm = dict(shared)
        mm["xT"] = np.ascontiguousarray(inp["x"][b].T, dtype=np.float32)
        in_maps.append(mm)
    res = run_bass_kernel_spmd(nc, in_maps, core_ids=list(range(8)))
    out = np.stack([np.ascontiguousarray(r["yT"].T) for r in res.results], axis=0)
    return out.astype(np.float32)
```

```python
import numpy as np
import ml_dtypes
from contextlib import ExitStack
import concourse.bass as bass
import concourse.mybir as mybir
from concourse.bass_utils import run_bass_kernel_spmd

F32 = mybir.dt.float32
BF16 = mybir.dt.bfloat16
AF = mybir.ActivationFunctionType
ALU = mybir.AluOpType
S = 2048
D = 1024
DIN = 2840
NEGM = -240000.0
EPS = 1e-6
WARM = 0
NDUM = 0
PDUM = 0
DVE_RECIP = (2,)
DVE_RECIP_MINC = 2
BF = ml_dtypes.bfloat16


class Sched:
    def __init__(self, nc, es):
        self.nc = nc
        self.es = es
        self.E = {}
        for name, obj in (("pe", nc.tensor), ("dve", nc.vector), ("act", nc.scalar),
                          ("pool", nc.gpsimd), ("sp", nc.sync)):
            sem = es.enter_context(nc.semaphore("s_" + name))
            self.E[name] = {"obj": obj, "sem": sem, "cnt": 0, "waited": {}, "name": name}
        self.lastw = {}
        self.readers = {}
        self.dsems = {}

    def _deps(self, E, reads, writes):
        toks = []
        me = E["name"]
        for r in reads:
            t = self.lastw.get(r)
            if t is not None and not (t[2] == "pe" and me == "pe"):
                toks.append(t)
            if is_psum_key(r):
                for t in self.readers.get(r, ()):
                    if t[2] != me:
                        toks.append(t)
        for w in writes:
            t = self.lastw.get(w)
            if t is not None and t[2] != me:
                toks.append(t)
            for t in self.readers.get(w, ()):
                if t[2] != me:
                    toks.append(t)
        need = {}
        for t in toks:
            k = id(t[0])
            if k not in need or need[k][1] < t[1]:
                need[k] = (t[0], t[1])
        for k, (sem, val) in need.items():
            if E["waited"].get(k, 0) < val:
                E["obj"].wait_ge(sem, val)
                E["waited"][k] = val

    def _commit(self, tok, reads, writes):
        for w in writes:
            self.lastw[w] = tok
            self.readers[w] = []
        for r in reads:
            self.readers.setdefault(r, []).append(tok)

    def op(self, eng, fn, reads=(), writes=()):
        E = self.E[eng]
        self._deps(E, reads, writes)
        ins = fn(E["obj"])
        E["cnt"] += 1
        ins.then_inc(E["sem"], 1)
        self._commit([E["sem"], E["cnt"], eng], reads, writes)

    def dma(self, q, out, in_, reads, writes, dkey):
        E = self.E[q]
        self._deps(E, reads, writes)
        if dkey not in self.dsems:
            self.dsems[dkey] = [self.es.enter_context(self.nc.semaphore("d_" + dkey)), 0, []]
        ds = self.dsems[dkey]
        ins = E["obj"].dma_start(out=out, in_=in_)
        ds[1] += 16
        ins.then_inc(ds[0], 16)
        tok = [ds[0], ds[1], "dma"]
        ds[2].append(tok)
        self._commit(tok, reads, writes)

    def finalize_group(self, dkey):
        ds = self.dsems[dkey]
        for t in ds[2]:
            t[1] = ds[1]

    def fence(self, keys):
        toks = [[E["sem"], E["cnt"], "fence_" + n] for n, E in self.E.items() if E["cnt"] > 0]
        for k in keys:
            self.readers[k] = list(toks) + list(self.readers.get(k, []))

    def wait_all_dma(self, q, dkeys):
        E = self.E[q]
        for dk in dkeys:
            ds = self.dsems[dk]
            E["obj"].wait_ge(ds[0], ds[1])


def is_psum_key(k):
    if isinstance(k, tuple):
        return k[0] in ("ps", "ors")
    return k in ("psG", "psI", "psT")


def slopes():
    return [2.0 ** (-(h + 1)) for h in range(8)]


def host_consts():
    c = {}
    ident = np.eye(128, dtype=np.float32)
    blockones = np.kron(np.eye(2), np.ones((64, 64))).astype(np.float32)
    ones = np.ones((128, 128), np.float32)
    i = np.arange(128)[:, None]
    j = np.arange(128)[None, :]
    tri = np.where(j >= i, 0.0, NEGM).astype(np.float32)
    atri = np.where(j < i, 0.0, NEGM).astype(np.float32)
    c["cb"] = np.concatenate([ident, blockones, ones, tri, atri], axis=1).astype(BF)
    n = np.arange(128)[:, None]
    t = np.arange(S)[None, :]
    c["mc"] = np.where(t >= 16 * n + 31, 0.0, NEGM).astype(BF)
    selh = np.zeros((128, 24, 128), np.float32)
    for k in range(24):
        selh[k, k, :] = 1.0
        selh[32 + k, k, :] = 1.0
    c["selh"] = selh.reshape(128, 24 * 128).astype(BF)
    cs = np.arange(127) * 16
    ce = cs + 32
    ss = np.arange(32) * 64
    se = ss + 64
    ov = np.clip(np.minimum(ce[:, None], se[None, :]) - np.maximum(cs[:, None], ss[None, :]), 0, None) / 32.0
    ova = np.zeros((128, 33), np.float32)
    ova[:127, :32] = ov
    ova[:127, 32] = 1.0
    c["ova"] = ova.astype(BF)
    k = np.arange(S)
    krs = np.zeros((37, S), np.float32)
    krs[:32] = (k[None, :] // 64 == np.arange(32)[:, None])
    krs[32] = 1.0
    krs[33] = 1.0
    krs[34] = k % 128
    krs[35] = 1.0
    krs[36] = k // 128
    krw = krs.copy()
    krw[:32] = 0.0
    krc = np.zeros((37, 254), np.float32)
    krc[32] = 1.0
    krc[33] = 1.0
    krc[34] = 16.0 * (np.arange(254) % 127)
    krc[35] = 1.0
    krc[36] = 31.0 / 128.0
    c["krs"] = krs.astype(BF)
    c["krw"] = krw.astype(BF)
    c["krc"] = krc.astype(BF)
    jj = k % 512
    qrows = np.zeros((8, 5, S), np.float32)
    for h, sl in enumerate(slopes()):
        qrows[h, 0] = -8.0 * sl * (jj % 256)
        qrows[h, 1] = -8.0 * sl * 256.0 * (jj // 256)
        qrows[h, 2] = 8.0 * sl
        qrows[h, 3] = -8.0 * sl * 512.0 * (k // 512)
        qrows[h, 4] = 8.0 * sl * 128.0
    c["qrows"] = qrows.astype(BF)
    bsw = np.zeros((128, 128), np.float32)
    bc = np.zeros((128, 32), np.float32)
    for h, sl in enumerate(slopes()):
        for dl in range(-3, 13):
            bsw[:, h * 16 + dl + 3] = -sl * 128.0 * dl
        for cc in range(4):
            bc[:, h * 4 + cc] = sl * (16.0 * np.arange(128) + 31.0 - 512.0 * cc)
    A = np.zeros((128, 16, 32), np.float32)
    Bm = np.zeros((128, 16, 32), np.float32)
    for tt in range(16):
        tpos = tt * 128 + np.arange(128)
        cur = (tpos // 64)[:, None]
        m = np.arange(32)[None, :]
        forced = (m == 0) | (m == cur) | (m == cur - 1)
        A[:, tt, :] = np.where(forced, 0.0, np.where(m <= cur, 1.0, 0.0))
        Bm[:, tt, :] = np.where(forced, 1.0e4, np.where(m <= cur, 0.0, -1.0))
    cv = np.zeros((128, 3), np.float32)
    cv[:, 0] = EPS
    cv[:, 1] = 1.0
    cv[:, 2] = 1e-30
    c["cf"] = np.concatenate([bsw, bc, A.reshape(128, 512), Bm.reshape(128, 512), cv], axis=1).astype(np.float32)
    return c


CF_BSW, CF_BC, CF_A, CF_B, CF_CV = 0, 128, 160, 672, 1184
CF_W = 1187
V_G, V_CW, V_CB, V_BA, V_BI, V_LAM, V_GQ, V_GKS, V_GKW, V_GKC = 0, 8, 24, 28, 32, 36, 40, 41, 42, 43
V_W = 44


def build(debug=(), stop_after=None, branches=(0, 1, 2)):
    nc = bass.Bass("TRN2", target_bir_lowering=False)
    es = ExitStack()
    with es:
        sc = Sched(nc, es)

        def din(name, shape, dt=F32):
            return nc.dram_tensor(name, list(shape), dt, kind="ExternalInput").ap()

        xT = din("xT", [D, S])
        w_in = din("w_in", [D, DIN])
        w_out = din("w_out", [D, D])
        vec = din("vec", [128, V_W])
        wa_d = din("wa", [128, 512])
        wi_d = din("wi", [128, 512])
        pos_d = din("posT2", [128, 16])
        w1k_d = din("w1k", [128, 4096])
        w1v_d = din("w1v", [128, 4096])
        w2k_d = din("w2k", [128, 128])
        w2v_d = din("w2v", [128, 128])
        cb_d = din("cb", [128, 640], BF16)
        mc_d = din("mc", [128, S], BF16)
        selh_d = din("selh", [128, 3072], BF16)
        ova_d = din("ova", [128, 33], BF16)
        krs_d = din("krs", [37, S], BF16)
        krw_d = din("krw", [37, S], BF16)
        krc_d = din("krc", [37, 254], BF16)
        qrows_d = din("qrows", [8, 5, S], BF16)
        cf_d = din("cf", [128, CF_W])
        yT = nc.dram_tensor("yT", [D, S], F32, kind="ExternalOutput").ap()

        def sb(name, shape, dt):
            return es.enter_context(nc.sbuf_tensor(name, list(shape), dt))

        def ps(name):
            return es.enter_context(nc.psum_tensor(name, [128, 512], F32))

        A1 = sb("A1", [128, 12288], F32)
        A2 = sb("A2", [128, 16384], BF16)
        A3 = sb("A3", [128, 6144], F32)
        Kc = sb("Kc", [128, 254], BF16)
        vctok = sb("vctok", [128, 256], BF16)
        vtok = sb("vtok", [128, 2 * 16 * 256], BF16)
        siluz = sb("siluz", [128, 4 * S], BF16)
        SIG = sb("SIG", [128, S], BF16)
        catT = sb("catT", [128, 8 * S], BF16)
        hidT = sb("hidT", [128, 2 * 508], BF16)
        T0 = sb("T0", [128, 1024], F32)
        T1 = sb("T1", [128, 1024], BF16)
        T2 = sb("T2", [128, 1024], F32)
        cb = sb("cbs", [128, 640], BF16)
        mc = sb("mcs", [128, S], BF16)
        selh = sb("selhs", [128, 3072], BF16)
        ova = sb("ovas", [128, 33], BF16)
        cf = sb("cfs", [128, CF_W], F32)
        vt = sb("vecs", [128, V_W], F32)
        wab = sb("wab", [128, 512], BF16)
        wib = sb("wib", [128, 512], BF16)
        posb = sb("posb", [128, 16], BF16)
        w2kb = sb("w2kb", [128, 128], BF16)
        w2vb = sb("w2vb", [128, 128], BF16)
        small = sb("small", [128, 64], F32)
        impacc = sb("impacc", [128, 128], F32)
        impt = sb("impt", [128, 128], F32)
        score = sb("score", [128, 128], F32)
        selt = sb("selt", [128, 256], BF16)
        VT = T0[:, :].bitcast(BF16)
        PS = [ps("ps%d" % i) for i in range(8)]

        ident = cb[:, 0:128]
        blockones = cb[:, 128:256]
        ones = cb[:, 256:384]
        TRI = cb[:, 384:512]
        ATRI = cb[:, 512:640]
        eps_ap = cf[:, CF_CV:CF_CV + 1]
        one_ap = cf[:, CF_CV + 1:CF_CV + 2]

        B = [A1[:, i * 2048:(i + 1) * 2048] for i in range(5)]
        XCb = [A1[:, 10240 + i * 1024:10240 + (i + 1) * 1024].bitcast(BF16) for i in range(2)]
        Qh = [A1[:, h * 1024:(h + 1) * 1024].bitcast(BF16) for h in range(8)]
        Kt = [A1[:, 8192 + i * 1024:8192 + (i + 1) * 1024].bitcast(BF16) for i in range(4)]
        xb = [A2[:, kc * 2048:(kc + 1) * 2048] for kc in range(8)]
        rstd = A3[:, 0:2048]
        wst = [A3[:, 2048 + b * 2048:2048 + (b + 1) * 2048].bitcast(BF16) for b in range(2)]
        KCB = [siluz[:, 0:2048], siluz[:, 2048:4096]]
        KB2 = [siluz[:, 4096:6144], siluz[:, 6144:8192]]
        w1kb = catT[:, 4 * S:6 * S]
        w1vb = catT[:, 6 * S:8 * S]

        def cload(q, dst, src, key):
            sc.dma(q, dst, src, (), (key,), "c_" + q)

        cload("sp", cb[:], cb_d, "cb")
        cload("sp", cf[:], cf_d, "cf")
        cload("sp", vt[:], vec, "vec")
        cload("act", mc[:], mc_d, "mc")
        cload("act", selh[:], selh_d, "selh")
        cload("act", ova[:], ova_d, "ova")
        cload("pool", wab[:], wa_d, "wab")
        cload("pool", wib[:], wi_d, "wib")
        cload("pool", posb[:], pos_d, "posb")
        cload("pool", w2kb[:], w2k_d, "w2kb")
        cload("pool", w2vb[:], w2v_d, "w2vb")
        cload("pool", w1kb, w1k_d, "w1kb")
        cload("pool", w1vb, w1v_d, "w1vb")
        sc.finalize_group("c_sp")
        sc.finalize_group("c_act")
        sc.finalize_group("c_pool")
        CONST = ("cb", "cf", "vec")

        def tgs(tg):
            return slice(tg * 512, (tg + 1) * 512)

        def rfast(e, out, in_):
            return e.reciprocal_approx_fast(out, in_)

        xTv = xT.rearrange("(kc p) t -> kc p t", p=128)
        Xs = [A1[:, kc * 2048:(kc + 1) * 2048] for kc in range(6)] + [A3[:, 2048:4096], A3[:, 4096:6144]]

        def xkeys(kc):
            if kc < 5:
                return tuple(("B", kc, tg) for tg in range(4))
            if kc == 5:
                return ("X5",)
            return (("wst", kc - 6),)

        for kc in range(8):
            sc.dma("sp", Xs[kc], xTv[kc], (), xkeys(kc), "xs%d" % kc)
        for kc in range(8):
            for tg in range(4):
                j = (kc * 4 + tg) % 2
                t1 = T1[:, j * 512:(j + 1) * 512]
                sc.op("act", lambda e: e.activation(t1, Xs[kc][:, tgs(tg)], AF.Square),
                      reads=xkeys(kc), writes=(("T1", j),))
                sc.op("pe", lambda e: e.matmul(PS[tg][:, :], ones, t1, start=(kc == 0), stop=(kc == 7)),
                      reads=(("T1", j), "cb"), writes=(("ps", tg),))
        for tg in range(4):
            sc.op("act", lambda e: e.activation(rstd[:, tgs(tg)], PS[tg][:, :], AF.Ln, bias=eps_ap, scale=1.0 / D),
                  reads=(("ps", tg), "cf"), writes=(("rstd", tg),))
            sc.op("act", lambda e: e.activation(rstd[:, tgs(tg)], rstd[:, tgs(tg)], AF.Exp, scale=-0.5),
                  reads=(("rstd", tg),), writes=(("rstd", tg),))
        for kc in (6, 7, 0, 1, 2, 3, 4, 5):
            for tg in range(4):
                sc.op("dve", lambda e: e.scalar_tensor_tensor(xb[kc][:, tgs(tg)], Xs[kc][:, tgs(tg)], vt[:, V_G + kc:V_G + kc + 1],
                                                              rstd[:, tgs(tg)], ALU.mult, ALU.mult),
                      reads=xkeys(kc) + (("rstd", tg), "vec"), writes=(("xb", kc),))

        dbg = {}
        if stop_after == "p1":
            dbg["rstd"] = (rstd, ("rstd", 3), [128, S], F32)
            dbg["xb0"] = (xb[0], ("xb", 0), [128, S], BF16)
            dbg["xb7"] = (xb[7], ("xb", 7), [128, S], BF16)
            return finish(nc, sc, es, yT, dbg, catT, debug)

        w_inv = w_in.rearrange("(kc p) c -> p kc c", p=128)
        groups = [(0, 512), (512, 512), (1024, 512), (1536, 512), (2048, 512), (2560, 280)]

        GBUF = {0: 0, 1: 1, 2: 0, 3: 2, 4: 1, 5: 2}
        wst.append(A3[:, 0:2048].bitcast(BF16))
        sc.fence([("wst", 2)])

        def load_group(gi):
            c0, n = groups[gi]
            b = GBUF[gi]
            dst = wst[b].rearrange("p (kc c) -> p kc c", kc=8)[:, :, 0:n]
            sc.dma("pool", dst, w_inv[:, :, c0:c0 + n], (), (("wst", b),), "wst%d" % b)

        load_group(0)
        load_group(1)
        load_group(3)
        sc.op("pool", lambda e: e.memset(SIG[:, :], 0.0), writes=tuple(("SIG", tg) for tg in range(4)))
        sc.op("pool", lambda e: e.memset(vtok[:, :], 1.0), writes=tuple(("vtok", w_, tg) for w_ in range(2) for tg in range(4)))
        sc.op("pool", lambda e: e.memset(vctok[:, :], 1.0), writes=("vctok",))
        sc.dma("sp", Kc[64:101, :], krc_d, (), ("Kcr",), "kcr")

        sc.op("act", lambda e: e.activation(small[:, 0:4], vt[:, V_LAM:V_LAM + 4], AF.Exp, scale=-1.0),
              reads=("vec",), writes=("cc",))
        sc.op("act", lambda e: e.activation(small[:, 0:4], small[:, 0:4], AF.Ln, bias=one_ap, scale=1.0),
              reads=("cc", "cf"), writes=("cc",))
        sc.op("dve", lambda e: e.tensor_scalar(small[:, 0:4], small[:, 0:4], -8.0, None, ALU.mult),
              reads=("cc",), writes=("cc",))

        XC = A1[:, 10240:11264].bitcast(BF16)
        sc.fence([("XCb", tg) for tg in range(4)])

        jobs = []

        def add_job(gi, off, ncols, tg, post, after=None, staged=False):
            jobs.append(dict(gi=gi, off=off, ncols=ncols, tg=tg, post=post, after=after, staged=staged))

        def post_x(c):
            def f(tg, P, pk):
                cw = lambda k: vt[:, V_CW + c * 4 + k:V_CW + c * 4 + k + 1]
                lo, hi = tg * 512, (tg + 1) * 512

                def st0():
                    sc.op("dve", lambda e: e.tensor_copy(B[0][:, lo:hi], P[:, :]), reads=(pk,), writes=(("B", 0, tg),))
                    sc.op("dve", lambda e: e.tensor_scalar(B[1][:, lo:hi], B[0][:, lo:hi], cw(3), vt[:, V_CB + c:V_CB + c + 1], ALU.mult, ALU.add),
                          reads=(("B", 0, tg), "vec"), writes=(("B", 1, tg),))
                    prev = ((("B", 0, tg - 1),) if tg > 0 else ())
                    for sh, k in ((1, 2), (2, 1), (3, 0)):
                        l2 = max(lo, sh)
                        sc.op("dve", lambda e, sh=sh, k=k, l2=l2: e.scalar_tensor_tensor(
                            B[1][:, l2:hi], B[0][:, l2 - sh:hi - sh], cw(k), B[1][:, l2:hi], ALU.mult, ALU.add),
                            reads=(("B", 0, tg), ("B", 1, tg), "vec") + prev, writes=(("B", 1, tg),))

                def st1():
                    sc.op("act", lambda e: e.activation(XC[:, lo:hi], B[1][:, lo:hi], AF.Copy), reads=(("B", 1, tg),), writes=(("XCb", tg),))
                    for wt, pidx in ((wab, 4), (wib, 5)):
                        sc.op("pe", lambda e, wt=wt, pidx=pidx: e.matmul(PS[pidx][:, :], wt[:, c * 128:(c + 1) * 128], XC[:, lo:hi],
                                                                         start=True, stop=True),
                              reads=(("XCb", tg), "wab", "wib"), writes=(("ps", pidx),))
                    sc.op("act", lambda e: e.activation(B[2][:, lo:hi], PS[4][:, :], AF.Sigmoid, bias=vt[:, V_BA + c:V_BA + c + 1], scale=1.0),
                          reads=(("ps", 4), "vec"), writes=(("B", 2, tg),))
                    sc.op("act", lambda e: e.activation(B[3][:, lo:hi], PS[5][:, :], AF.Sigmoid, bias=vt[:, V_BI + c:V_BI + c + 1], scale=1.0),
                          reads=(("ps", 5), "vec"), writes=(("B", 3, tg),))
                    sc.op("pool", lambda e: e.tensor_tensor(B[3][:, lo:hi], B[3][:, lo:hi], B[1][:, lo:hi], ALU.mult),
                          reads=(("B", 3, tg), ("B", 1, tg)), writes=(("B", 3, tg),))

                def st2():
                    sc.op("act", lambda e: e.activation(B[2][:, lo:hi], B[2][:, lo:hi], AF.Exp, scale=small[:, c:c + 1]),
                          reads=(("B", 2, tg), "cc"), writes=(("B", 2, tg),))
                    sc.op("dve", lambda e: e.tensor_tensor(B[4][:, lo:hi], B[2][:, lo:hi], B[2][:, lo:hi], ALU.mult),
                          reads=(("B", 2, tg),), writes=(("B", 4, tg),))

                def st3():
                    sc.op("act", lambda e: e.activation(B[4][:, lo:hi], B[4][:, lo:hi], AF.Ln, bias=one_ap, scale=-1.0),
                          reads=(("B", 4, tg), "cf"), writes=(("B", 4, tg),))
                    sc.op("act", lambda e: e.activation(B[4][:, lo:hi], B[4][:, lo:hi], AF.Exp, scale=0.5),
                          reads=(("B", 4, tg),), writes=(("B", 4, tg),))
                    sc.op("pool", lambda e: e.tensor_tensor(B[3][:, lo:hi], B[3][:, lo:hi], B[4][:, lo:hi], ALU.mult),
                          reads=(("B", 3, tg), ("B", 4, tg)), writes=(("B", 3, tg),))

                def st4():
                    if tg > 0:
                        sc.op("dve", lambda e: e.scalar_tensor_tensor(B[3][:, lo:lo + 1], B[2][:, lo:lo + 1], B[4][:, lo - 1:lo],
                                                                      B[3][:, lo:lo + 1], ALU.mult, ALU.add),
                              reads=(("B", 2, tg), ("B", 3, tg), ("B", 4, tg - 1)), writes=(("B", 3, tg),))
                    sc.op("dve", lambda e: e.tensor_tensor_scan(B[4][:, lo:hi], B[2][:, lo:hi], B[3][:, lo:hi], 0.0, ALU.mult, ALU.add),
                          reads=(("B", 2, tg), ("B", 3, tg)), writes=(("B", 4, tg),))
                return [st0, None, st1, None, st2, None, st3, None, st4]
            return f

        def post_z(c):
            def f(tg, P, pk):
                j = tg % 2
                lo, hi = tg * 512, (tg + 1) * 512
                zt = T0[:, j * 512:(j + 1) * 512] if tg < 2 else A1[:, 11264 + j * 512:11264 + (j + 1) * 512]
                zk = ("T0", j) if tg < 2 else ("ZS", j)

                def st0():
                    sc.op("act", lambda e: e.activation(zt, P[:, :], AF.Sigmoid), reads=(pk,), writes=(zk,))
                    sc.op("dve", lambda e: e.tensor_tensor(zt, zt, P[:, :], ALU.mult), reads=(pk, zk), writes=(zk,))

                def st2():
                    sc.op("pool", lambda e: e.tensor_tensor(catT[:, c * S + lo:c * S + hi], B[4][:, lo:hi], zt, ALU.mult),
                          reads=(("B", 4, tg), zk), writes=(("cat", c),))
                return [st0, None, None, None, None, None, st2]
            return f

        def setup_qk():
            qkeys = ([("Qd", h, tg) for h in range(8) for tg in range(4)] + [("Qs", h, c) for h in range(8) for c in range(4)]
                     + [("Qr", h) for h in range(8)] + [("Kd", i, tg) for i in range(4) for tg in range(4)]
                     + [("Kr", i) for i in range(4)])
            sc.fence(qkeys)
            for h in range(8):
                sc.op("pool", lambda e, h=h: e.memset(Qh[h][64:96, :], 0.0), writes=tuple(("Qs", h, c) for c in range(4)))
                sc.dma("sp", Qh[h][96:101, :], qrows_d[h], (), (("Qr", h),), "qr")
            for i in range(4):
                sc.dma("sp", Kt[i][64:101, :], krs_d if i < 2 else krw_d, (), (("Kr", i),), "qr")
            sc.finalize_group("qr")

        def post_norm(gcol, outs_fn):
            def f(tg, P, pk):
                j = tg % 2
                fs = slice(j * 512, (j + 1) * 512)
                t1, t2 = T1[:, fs], T2[:, fs]

                def st0():
                    sc.op("act", lambda e: e.activation(t1, P[:, :], AF.Square), reads=(pk,), writes=(("T1", j),))
                    sc.op("pe", lambda e: e.matmul(PS[4 + j][:, :], blockones, t1, start=True, stop=True),
                          reads=(("T1", j), "cb"), writes=(("ps", 4 + j),))

                def st1():
                    sc.op("act", lambda e: e.activation(t2, PS[4 + j][:, :], AF.Ln, bias=eps_ap, scale=1.0 / 64),
                          reads=(("ps", 4 + j), "cf"), writes=(("T2", j),))
                    sc.op("act", lambda e: e.activation(t2, t2, AF.Exp, scale=-0.5), reads=(("T2", j),), writes=(("T2", j),))
                    for half, (dst, key) in enumerate(outs_fn(tg)):
                        rows = slice(half * 64, (half + 1) * 64)
                        sc.op("dve", lambda e, dst=dst, rows=rows: e.scalar_tensor_tensor(
                            dst, P[rows, :], vt[rows, gcol:gcol + 1], t2[rows, :], ALU.mult, ALU.mult),
                            reads=(pk, ("T2", j), "vec"), writes=(key,))
                return [st0, st1]
            return f

        PS6b = PS[6][:, 0:256].bitcast(BF16)

        def post_v(which):
            def f(tg, P, pk):
                jv = tg % 2
                vts = T1[:, jv * 512:(jv + 1) * 512]
                sc.op("dve", lambda e: e.tensor_copy(vts, P[:, :]), reads=(pk,), writes=(("T1", jv),))
                for q4 in range(4):
                    sc.op("pe", lambda e, q4=q4: e.transpose(PS6b[:, q4 * 128:(q4 + 1) * 128], vts[:, q4 * 128:(q4 + 1) * 128], ident),
                          reads=(("T1", jv), "cb"), writes=(("ps", 6),))
                base = which * 4096 + tg * 1024
                dst = vtok[:, base:base + 1024].rearrange("p (k g x) -> p k g x", k=4, g=2)[:, :, :, 0:64]
                src = PS6b[:, 0:512].rearrange("p (k g d) -> p k g d", k=4, g=2)
                sc.op("act", lambda e: e.activation(dst, src, AF.Copy), reads=(("ps", 6),), writes=(("vtok", which, tg),))
            return f

        def k_outs(i0):
            return lambda tg: [(Kt[i0][0:64, tgs(tg)], ("Kd", i0, tg)), (Kt[i0 + 1][0:64, tgs(tg)], ("Kd", i0 + 1, tg))]


        def compression():
            w1v_ = [w1kb.rearrange("p (c h) -> p c h", c=16), w1vb.rearrange("p (c h) -> p c h", c=16)]
            w2v_ = [w2kb.rearrange("p (a d) -> p a d", a=2), w2vb.rearrange("p (a d) -> p a d", a=2)]
            for kv in range(2):
                for half in range(2):
                    col = 400 + kv * 2 + half
                    for c in range(16):
                        sc.op("pe", lambda e, c=c, col=col, kv=kv, half=half: e.matmul(
                            PS[7][:, col:col + 1], w1v_[kv][:, c, half * 128:(half + 1) * 128], posb[:, c:c + 1],
                            start=(c == 0), stop=(c == 15)),
                            reads=("w1kb", "w1vb", "posb"), writes=(("ps", 7),))
            sc.op("dve", lambda e: e.tensor_copy(small[:, 4:8], PS[7][:, 400:404]), reads=(("ps", 7),), writes=("bpos",))
            yield
            for kv in range(2):
                for g in range(2):
                    rows = slice(g * 64, (g + 1) * 64)
                    src = KCB[kv][rows, :].rearrange("p (n s) -> p s n", s=16)
                    xc = KB2[g][:, 0:2032].rearrange("p (c n) -> p c n", c=16)
                    for lpar in range(2):
                        prow = slice(lpar * 64, (lpar + 1) * 64)
                        for hi_, eng in ((0, "dve"), (1, "dve")):
                            sc.op(eng, lambda e, lpar=lpar, prow=prow, hi_=hi_: e.tensor_copy(
                                xc[prow, hi_ * 8:(hi_ + 1) * 8, :], src[:, lpar::2, hi_:hi_ + 127]),
                                reads=(("KCB", kv),), writes=(("KB2", g),))
                    yield
                for g in range(2):
                    for half in range(2):
                        o0 = (half * 2 + g) * 127
                        for c in range(16):
                            sc.op("pe", lambda e, c=c, g=g, o0=o0, half=half: e.matmul(
                                PS[6][:, o0:o0 + 127], w1v_[kv][:, c, half * 128:(half + 1) * 128],
                                KB2[g][:, c * 127:(c + 1) * 127], start=(c == 0), stop=(c == 15)),
                                reads=(("KB2", g), "w1kb", "w1vb"), writes=(("ps", 6),))
                for half in range(2):
                    sc.op("act", lambda e, half=half: e.activation(
                        hidT[:, kv * 508 + half * 254:kv * 508 + (half + 1) * 254], PS[6][:, half * 254:(half + 1) * 254],
                        AF.Silu, bias=small[:, 4 + kv * 2 + half:5 + kv * 2 + half], scale=1.0),
                        reads=(("ps", 6), "bpos"), writes=(("hidT", kv),))
                if kv == 0:
                    for g in range(2):
                        for half in range(2):
                            sc.op("pe", lambda e, g=g, half=half: e.matmul(
                                PS[7][0:64, g * 127:(g + 1) * 127], w2v_[0][:, half, :],
                                hidT[:, half * 254 + g * 127:half * 254 + (g + 1) * 127], start=(half == 0), stop=(half == 1)),
                                reads=(("hidT", 0), "w2kb"), writes=(("ps", 7),))
                    sc.op("act", lambda e: e.activation(T1[0:64, 0:254], PS[7][0:64, 0:254], AF.Square),
                          reads=(("ps", 7),), writes=(("T1", 0),))
                    sc.op("pe", lambda e: e.matmul(PS[4][0:64, 0:254], blockones[0:64, 0:64], T1[0:64, 0:254], start=True, stop=True),
                          reads=(("T1", 0), "cb"), writes=(("ps", 4),))
                    sc.op("act", lambda e: e.activation(T2[0:64, 0:254], PS[4][0:64, 0:254], AF.Ln, bias=eps_ap[0:64, :], scale=1.0 / 64),
                          reads=(("ps", 4), "cf"), writes=(("T2", 0),))
                    sc.op("act", lambda e: e.activation(T2[0:64, 0:254], T2[0:64, 0:254], AF.Exp, scale=-0.5),
                          reads=(("T2", 0),), writes=(("T2", 0),))
                    sc.op("dve", lambda e: e.scalar_tensor_tensor(Kc[0:64, 0:254], PS[7][0:64, 0:254], vt[0:64, V_GKC:V_GKC + 1],
                                                                  T2[0:64, 0:254], ALU.mult, ALU.mult),
                          reads=(("ps", 7), ("T2", 0), "vec"), writes=("Kcd",))
                else:
                    for g in range(2):
                        for half in range(2):
                            sc.op("pe", lambda e, g=g, half=half: e.matmul(
                                PS[7][0:127, 256 + g * 64:256 + (g + 1) * 64],
                                hidT[:, 508 + half * 254 + g * 127:508 + half * 254 + (g + 1) * 127], w2v_[1][:, half, :],
                                start=(half == 0), stop=(half == 1)),
                                reads=(("hidT", 1), "w2vb"), writes=(("ps", 7),))
                    sc.op("act", lambda e: e.activation(vctok[0:127, :].rearrange("p (g x) -> p g x", g=2)[:, :, 0:64],
                                                        PS[7][0:127, 256:384].rearrange("p (g d) -> p g d", g=2), AF.Copy),
                          reads=(("ps", 7),), writes=("vctok",))
            sc.fence([("sz", p_, tg) for p_ in range(4) for tg in range(4)])

        extras = []
        for kv in range(2):
            for tg in range(4):
                def fkc(tg, P, pk, kv=kv):
                    sc.op("dve", lambda e: e.tensor_copy(KCB[kv][:, tgs(tg)], P[:, :]), reads=(pk,), writes=(("KCB", kv),))
                extras.append((3, kv * 128, 128, tg, fkc))
        for tg in range(4):
            extras.append((3, 384, 128, tg, post_v(0)))
        cgen = compression()
        ei = 0
        csteps = 0
        for c in range(4):
            for slot in (("x", 0), ("x", 1), ("z", 0), "E", ("x", 2), ("z", 1), "E", ("x", 3), ("z", 2), ("z", 3), "E"):
                if slot == "E":
                    add_job(*extras[ei])
                    ei += 1
                else:
                    kind, tg = slot
                    last = (c == 3 and tg == 3)
                    if kind == "x":
                        add_job(0, c * 128, 128, tg, post_x(c), after=((lambda: load_group(2)) if last else None), staged=True)
                    else:
                        add_job(1, c * 128, 128, tg, post_z(c), after=((lambda: load_group(4)) if last else None), staged=True)
                        if last:
                            def lru_done():
                                for _ in cgen:
                                    pass
                                setup_qk()
                            jobs[-1]["after_post"] = lru_done
                if ei > 8 and csteps < 7 and not jobs[-1].get("after_post"):
                    jobs[-1]["after_post"] = (lambda: next(cgen, None))
                    csteps += 1
        for tg in range(4):
            add_job(4, 128, 128, tg, post_v(1))
        for _ in range(1):
            add_job(2, 0, 0, 0, (lambda tg, P, pk: None))
        for tg in range(4):
            add_job(3, 256, 128, tg, post_norm(V_GKS, k_outs(0)), after=((lambda: load_group(5)) if tg == 3 else None), staged=True)
        for p_ in range(4):
            for tg in range(4):
                outs = (lambda p_: (lambda tg: [(Qh[2 * p_][0:64, tgs(tg)], ("Qd", 2 * p_, tg)),
                                                (Qh[2 * p_ + 1][0:64, tgs(tg)], ("Qd", 2 * p_ + 1, tg))]))(p_)
                add_job(2, p_ * 128, 128, tg, post_norm(V_GQ, outs), staged=True)
        for tg in range(4):
            add_job(4, 0, 128, tg, post_norm(V_GKW, k_outs(2)), staged=True)

        for p_ in range(4):
            gi, off = (4, 256 + p_ * 128) if p_ < 2 else (5, (p_ - 2) * 128)
            for tg in range(4):
                def f(tg, P, pk, p_=p_):
                    sc.op("act", lambda e: e.activation(siluz[:, p_ * S + tg * 512:p_ * S + (tg + 1) * 512], P[:, :], AF.Silu),
                          reads=(pk,), writes=(("sz", p_, tg),))
                add_job(gi, off, 128, tg, f)

        for tg in range(4):
            def f(tg, P, pk):
                j = tg % 2
                t2 = T2[0:24, j * 512:(j + 1) * 512]
                sc.op("act", lambda e: e.activation(t2, P[0:24, :], AF.Sigmoid), reads=(pk,), writes=(("T2", j),))
                sc.op("act", lambda e: e.activation(SIG[0:24, tgs(tg)], t2, AF.Copy), reads=(("T2", j),), writes=(("SIG", tg),))
                sc.op("dve", lambda e: e.tensor_tensor(SIG[32:56, tgs(tg)], t2, SIG[0:24, tgs(tg)], ALU.subtract),
                      reads=(("T2", j), ("SIG", tg)), writes=(("SIG", tg),))
            add_job(5, 256, 24, tg, f)

        def emit_proj(ji):
            jb = jobs[ji]
            if jb.get("pre"):
                jb["pre"]()
            b = GBUF[jb["gi"]]
            wv = wst[b].rearrange("p (kc c) -> p kc c", kc=8)
            pi = ji % 4
            tg, off, ncols = jb["tg"], jb["off"], jb["ncols"]
            for kc in range(8 if ncols else 0):
                sc.op("pe", lambda e, kc=kc: e.matmul(PS[pi][0:ncols, :], wv[:, kc, off:off + ncols], xb[kc][:, tg * 512:(tg + 1) * 512],
                                                      start=(kc == 0), stop=(kc == 7)),
                      reads=(("wst", b), ("xb", kc)), writes=(("ps", pi),))
            if ji >= 36 and PDUM:
                sc.op("pe", lambda e: e.matmul(PS[7][:, 0:PDUM], ident, mc[:, 0:PDUM], start=True, stop=True),
                      reads=("cb", "mc"), writes=(("ps", 7),))
            if jb.get("after"):
                jb["after"]()

        LOOK = 2
        MAXS = 9
        PAIR_END = 50
        assert jobs[49]["gi"] == 3 and jobs[48]["ncols"] == 0

        def prep_job(t):
            jb_ = jobs[t]
            if jb_.get("staged"):
                jb_["stages"] = jb_["post"](jb_["tg"], PS[t % 4], ("ps", t % 4))
            else:
                jb_["stages"] = [(lambda jb_=jb_, t=t: jb_["post"](jb_["tg"], PS[t % 4], ("ps", t % 4)))]

        for ji in range(min(LOOK, len(jobs))):
            emit_proj(ji)
        t = 0
        while t < len(jobs) + MAXS:
            ts_ = (t, t + 1) if t < PAIR_END else (t,)
            for t_ in ts_:
                if t_ + LOOK < len(jobs):
                    emit_proj(t_ + LOOK)
            for t_ in ts_:
                if t_ < len(jobs):
                    prep_job(t_)
            done = []
            for s_ in reversed(range(MAXS)):
                for t_ in ts_:
                    j_ = t_ - s_
                    if 0 <= j_ < len(jobs):
                        stg = jobs[j_]["stages"]
                        if s_ < len(stg) and stg[s_] is not None:
                            stg[s_]()
                        if s_ == max(len(stg) - 1, 0):
                            done.append(j_)
            for j_ in sorted(done):
                if jobs[j_].get("after_post"):
                    jobs[j_]["after_post"]()
            t += len(ts_)

        if stop_after == "lru":
            dbg["lru_out"] = (catT[:, 0:4 * S], ("cat", 3), [128, 4 * S], BF16)
            return finish(nc, sc, es, yT, dbg, catT, debug)
        if stop_after == "proj":
            dbg["Q0"] = (Qh[0], ("Qd", 0, 3), [128, S], BF16)
            dbg["Q5"] = (Qh[5], ("Qd", 5, 3), [128, S], BF16)
            dbg["Ks1"] = (Kt[1], ("Kd", 1, 3), [128, S], BF16)
            dbg["Kw0"] = (Kt[2], ("Kd", 2, 3), [128, S], BF16)
            dbg["Kc"] = (Kc[:, :], "Kcd", [128, 254], BF16)
            dbg["vctok"] = (vctok[:, :], "vctok", [128, 256], BF16)
            dbg["vtok"] = (vtok[:, :], ("vtok", 1, 3), [128, 8192], BF16)
            dbg["siluz"] = (siluz[:, :], ("sz", 3, 3), [128, 4 * S], BF16)
            dbg["SIG"] = (SIG[:, :], ("SIG", 3), [128, S], BF16)
            return finish(nc, sc, es, yT, dbg, catT, debug)

        w_outb = A2[:, 0:8192].rearrange("p (kc c) -> p kc c", kc=8)
        PT = [A2[:, 8192 + r * 512:8192 + (r + 1) * 512] for r in range(3)]
        Rt = A2[:, 9728:10752].bitcast(F32)
        Wt = A2[:, 10752:11776].bitcast(F32)
        Tm = A2[:, 11776:12800].bitcast(F32)
        TS = [(Rt, Wt, Tm), (A2[:, 12800:13824].bitcast(F32), A2[:, 13824:14848].bitcast(F32), A2[:, 14848:15872].bitcast(F32))]
        acc = [A3[:, h * 512:(h + 1) * 512] for h in range(8)]
        sc.fence(["w_outb"] + [("PT", r) for r in range(3)] + [("Rt", j) for j in range(2)] + [("Wt", j) for j in range(2)]
                 + [("Tm", j) for j in range(2)] + [("acc", h) for h in range(8)]
                 + [("cat", 4 + p_, c) for p_ in range(4) for c in range(4)]
                 + [("ors", o_, hf_) for o_ in range(3) for hf_ in range(2)] + ["psG", "psI"])
        sc.dma("pool", w_outb, w_out.rearrange("(kc p) c -> p kc c", p=128), (), ("w_outb",), "w_outb")

        for i_ in range(WARM):
            sc.op("pe", lambda e: e.matmul(PS[0][:, :], ident, mc[:, 0:512], start=True, stop=True),
                  reads=("cb", "mc"), writes=(("ps", 0),))
        PS7b = PS[6][:, 256:320].bitcast(BF16)
        selh_v = selh[:, :].rearrange("p (k d) -> p k d", k=24)
        ectr = [0]
        PS6v = PS[6][:, 0:132].rearrange("p (t m) -> p t m", m=33)
        impacc_v = impacc[:, :].rearrange("p (t m) -> p t m", m=32)
        impt_v = impt[:, :].rearrange("p (t m) -> p t m", m=32)
        score_v = score[:, :].rearrange("p (t m) -> p t m", m=32)
        rI = small[:, 8:12]
        m8 = small[:, 16:48]
        gctr = [0]
        ORSB = [3, 4, 7]
        NORS = [2]

        def epilogue(h, c, br, o, last_br):
            cols = slice(c * 512, (c + 1) * 512)
            j = ectr[0] % 2
            ectr[0] += 1
            Rt_, Wt_, Tm_ = TS[j]
            P_ = slice(0, 64)
            R_ = slice(64, 128)
            ORS = PS[ORSB[o]]
            use_dve = br in DVE_RECIP and c >= DVE_RECIP_MINC and not (br == 0 and c == 0)
            if use_dve:
                Osrc, okeys = ORS[P_, :], (("ors", o, 0),)
            else:
                Osrc, okeys = T0[P_, j * 512:(j + 1) * 512], (("T0", j),)
                sc.op("dve", lambda e: e.tensor_copy(Osrc, ORS[P_, :]), reads=(("ors", o, 0),), writes=okeys)
            sc.op("pe", lambda e: e.matmul(PS[5][:, :], selh_v[:, 3 * h + (0, 2, 1)[br], :], SIG[:, cols], start=True, stop=True),
                  reads=(("SIG", c), "selh"), writes=("psG",))
            if br == 0 and c == 0:
                sc.op("act", lambda e: e.activation(Rt_[R_, :], ORS[R_, :], AF.Ln, bias=cf[R_, CF_CV + 2:CF_CV + 3], scale=1.0),
                      reads=(("ors", o, 1), "cf"), writes=(("Rt", j),))
            elif not (br in DVE_RECIP and c >= DVE_RECIP_MINC):
                sc.op("act", lambda e: e.activation(Rt_[R_, :], ORS[R_, :], AF.Ln), reads=(("ors", o, 1),), writes=(("Rt", j),))
            if br in DVE_RECIP and c >= DVE_RECIP_MINC and not (br == 0 and c == 0):
                sc.op("dve", lambda e: e.reciprocal(Rt_[R_, :], ORS[R_, :]), reads=(("ors", o, 1),), writes=(("Rt", j),))
            else:
                sc.op("act", lambda e: e.activation(Rt_[R_, :], Rt_[R_, :], AF.Exp, scale=-1.0), reads=(("Rt", j),), writes=(("Rt", j),))
            sc.op("dve", lambda e: e.tensor_tensor(Wt_[P_, :], Rt_[R_, :], PS[5][R_, :], ALU.mult),
                  reads=(("Rt", j), "psG"), writes=(("Wt", j),))
            if br not in branches:
                pass
            elif br == branches[0]:
                sc.op("dve", lambda e: e.tensor_tensor(acc[h][P_, :], Osrc, Wt_[P_, :], ALU.mult),
                      reads=okeys + (("Wt", j),), writes=(("acc", h),))
            else:
                sc.op("dve", lambda e: e.tensor_tensor(Tm_[P_, :], Osrc, Wt_[P_, :], ALU.mult),
                      reads=okeys + (("Wt", j),), writes=(("Tm", j),))
                sc.op("pool", lambda e: e.tensor_tensor(acc[h][P_, :], acc[h][P_, :], Tm_[P_, :], ALU.add),
                      reads=(("acc", h), ("Tm", j)), writes=(("acc", h),))
            if last_br:
                pr = h // 2
                cs = slice((4 + pr) * S + c * 512, (4 + pr) * S + (c + 1) * 512)
                zs = slice(pr * S + c * 512, pr * S + (c + 1) * 512)
                if h % 2 == 0:
                    sc.op("pool", lambda e: e.tensor_tensor(catT[P_, cs], acc[h][P_, :], siluz[P_, zs], ALU.mult),
                          reads=(("acc", h), ("sz", pr, c)), writes=(("cat", 4 + pr, c),))
                else:
                    sc.op("pool", lambda e: e.tensor_copy(acc[h][R_, :], acc[h][P_, :]), reads=(("acc", h),), writes=(("acc", h),))
                    sc.op("pool", lambda e: e.tensor_tensor(catT[R_, cs], acc[h][R_, :], siluz[R_, zs], ALU.mult),
                          reads=(("acc", h), ("sz", pr, c)), writes=(("cat", 4 + pr, c),))

        xre = [A3[:, 4096:4608], A3[:, 4608:5120]]
        ost = [A3[:, 5120:5632], A3[:, 5632:6144]]
        sc.fence([("ost", 0), ("ost", 1), ("xre", 0), ("xre", 1)])
        octr = [0]

        def outproj_slots(tc, banks, gap, nbuf=2):
            tcs = slice(tc * 512, (tc + 1) * 512)
            if nbuf > 2:
                for i_ in range(nbuf):
                    sc.dma("sp", xre[i_], xT[i_ * 128:(i_ + 1) * 128, tcs], (), (("xre", i_),), "xre%d" % i_)
            for cc in range(8):
                i2 = (octr[0] % 2) if nbuf == 2 else cc
                bk = banks[octr[0] % len(banks)]
                octr[0] += 1
                bkeys = (("ors", 2, 0), ("ors", 2, 1)) if bk == 7 else (("ps", bk),)
                for kc in range(8):
                    def mm(kc=kc, cc=cc, i2=i2, bk=bk):
                        if kc == 0 and nbuf == 2:
                            sc.dma("sp", xre[i2], xT[cc * 128:(cc + 1) * 128, tcs], (), (("xre", i2),), "xre%d" % i2)
                        rkeys = (("cat", kc),) if kc < 4 else (("cat", kc, tc),)
                        sc.op("pe", lambda e: e.matmul(PS[bk][:, :], w_outb[:, kc, cc * 128:(cc + 1) * 128],
                                                       catT[:, kc * S + tc * 512:kc * S + (tc + 1) * 512],
                                                       start=(kc == 0), stop=(kc == 7)),
                              reads=("w_outb",) + rkeys, writes=bkeys)
                    yield mm

                def ev(cc=cc, i2=i2, bk=bk):
                    sc.op("dve", lambda e: e.tensor_tensor(ost[i2], PS[bk][:, :], xre[i2], ALU.add),
                          reads=bkeys + (("xre", i2),), writes=(("ost", i2),))
                    sc.dma("sp", yT[cc * 128:(cc + 1) * 128, tcs], ost[i2], (("ost", i2),), (), "ost%d" % i2)
                yield ev
                for _ in range(gap):
                    yield None

        for c in range(4):
            cols = slice(c * 512, (c + 1) * 512)
            NORS[0] = 3 if c < 2 else 2
            if c < 2:
                filler = iter(())
            elif c == 2:
                filler = outproj_slots(0, [7], 8)
            else:
                def two_():
                    for a_ in outproj_slots(1, [7], 3):
                        yield a_
                    for a_ in outproj_slots(2, [7], 3):
                        yield a_
                filler = two_()
            units = []

            def branch_units(kind, br, h):
                ul = []
                if kind == "w":
                    for b_ in range(4):
                        kt = 4 * c - 4 + b_
                        if kt >= 0:
                            ul.append(dict(kt=kt, qlo=0, qhi=128 * (b_ + 1), mask=("atri", 128 * b_)))
                else:
                    for kt in range(4 * c):
                        ul.append(dict(kt=kt, qlo=0, qhi=512, mask=None))
                for a_ in range(4):
                    ul.append(dict(kt=4 * c + a_, qlo=128 * a_, qhi=512, mask=("tri", 128 * a_)))
                merged = []
                cur = None
                for i_, u in enumerate(ul):
                    n_ = u["qhi"] - u["qlo"]
                    u["sfirst"] = (i_ == 0)
                    u["slast"] = (i_ == len(ul) - 1)
                    if cur is not None and cur["tot"] + n_ <= 512:
                        u["off"] = cur["tot"]
                        cur["subs"].append(u)
                        cur["tot"] += n_
                    else:
                        u["off"] = 0
                        cur = dict(subs=[u], tot=n_)
                        merged.append(cur)
                for i_, m in enumerate(merged):
                    m.update(kind=kind, h=h, br=br, first=(i_ == 0), last=(i_ == len(merged) - 1))
                return merged

            for h in range(8):
                units.append(dict(kind="c", h=h, first=True, last=True, br=0))
                units += branch_units("w", 1, h)
            for h in range(8):
                units += branch_units("s", 2, h)

            def emit_S(u, r):
                h = u["h"]
                g = h // 4
                if u["kind"] == "c":
                    sc.op("pe", lambda e: e.matmul(PS[r][0:127, :], Kc[0:101, g * 127:(g + 1) * 127], Qh[h][0:101, cols],
                                                   start=True, stop=False),
                          reads=("Kcd", "Kcr", ("Qd", h, c), ("Qs", h, c), ("Qr", h)), writes=(("ps", r),))
                    sc.op("pe", lambda e: e.matmul(PS[r][0:127, :], ident[0:127, 0:127], mc[0:127, cols], start=False, stop=True),
                          reads=("cb", "mc"), writes=(("ps", r),))
                    return
                ki = g if u["kind"] == "s" else 2 + g
                for sb_ in u["subs"]:
                    kt, qlo, qhi, off = sb_["kt"], sb_["qlo"], sb_["qhi"], sb_["off"]
                    n_ = qhi - qlo
                    sc.op("pe", lambda e: e.matmul(PS[r][:, off:off + n_], Kt[ki][0:101, kt * 128:(kt + 1) * 128],
                                                   Qh[h][0:101, c * 512 + qlo:c * 512 + qhi], start=True, stop=(sb_["mask"] is None)),
                          reads=(("Kd", ki, kt // 4), ("Kr", ki), ("Qd", h, c), ("Qs", h, c), ("Qr", h)), writes=(("ps", r),))
                    if sb_["mask"] is not None:
                        mt, mlo = sb_["mask"]
                        msk = TRI if mt == "tri" else ATRI
                        mo = off + (mlo - qlo)
                        sc.op("pe", lambda e: e.matmul(PS[r][:, mo:mo + 128], ident, msk, start=False, stop=True),
                              reads=("cb",), writes=(("ps", r),))

            def emit_exp(u, r):
                h = u["h"]
                if u["kind"] == "c":
                    sc.op("act", lambda e: e.activation(PT[r][0:127, :], PS[r][0:127, :], AF.Exp, scale=0.125),
                          reads=(("ps", r),), writes=(("PT", r),))
                    return
                tot = u["tot"]
                sc.op("act", lambda e: e.activation(PT[r][:, 0:tot], PS[r][:, 0:tot], AF.Exp, scale=0.125),
                      reads=(("ps", r),), writes=(("PT", r),))

            def emit_PV(u, r):
                h = u["h"]
                g = h // 4
                pb = (h % 2) * 64
                rb = 64 - pb
                if u["first"]:
                    gctr[0] += 1
                u["o"] = gctr[0] % NORS[0]
                o = u["o"]
                ORS = PS[ORSB[o]]
                if u["kind"] == "c":
                    sc.op("pe", lambda e: e.matmul(ORS[:, :], vctok[0:127, g * 128:(g + 1) * 128], PT[r][0:127, :],
                                                   start=True, stop=True),
                          reads=(("PT", r), "vctok"), writes=(("ors", o, 0), ("ors", o, 1)))
                    for tt in range(4):
                        sc.op("pe", lambda e, tt=tt: e.matmul(PS[6][:, tt * 33:(tt + 1) * 33], PT[r][0:127, tt * 128:(tt + 1) * 128],
                                                              ova[0:127, 0:33], start=True, stop=True),
                              reads=(("PT", r), "ova"), writes=("psI",))
                    jh = h % 4
                    sc.op("dve", lambda e: e.tensor_scalar(rI, PS6v[:, :, 32], 1e-30, None, ALU.max),
                          reads=("psI",), writes=("rI",))
                    sc.op("dve", lambda e: e.reciprocal(rI, rI), reads=("rI",), writes=("rI",))
                    rIb = rI.unsqueeze(2).to_broadcast([128, 4, 32])
                    if jh == 0:
                        sc.op("dve", lambda e: e.tensor_tensor(impacc_v, PS6v[:, :, 0:32], rIb, ALU.mult),
                              reads=("psI", "rI"), writes=("impacc",))
                    else:
                        sc.op("dve", lambda e: e.tensor_tensor(impt_v, PS6v[:, :, 0:32], rIb, ALU.mult),
                              reads=("psI", "rI"), writes=("impt",))
                        sc.op("dve", lambda e: e.tensor_tensor(impacc[:, :], impacc[:, :], impt[:, :], ALU.add),
                              reads=("impacc", "impt"), writes=("impacc",))
                    if jh == 3:
                        a0 = CF_A + c * 128
                        b0 = CF_B + c * 128
                        sc.op("dve", lambda e: e.tensor_tensor(score[:, :], impacc[:, :], cf[:, a0:a0 + 128], ALU.mult),
                              reads=("impacc", "cf"), writes=("score",))
                        sc.op("dve", lambda e: e.tensor_tensor(score[:, :], score[:, :], cf[:, b0:b0 + 128], ALU.add),
                              reads=("score", "cf"), writes=("score",))
                        for tt in range(4):
                            sc.op("dve", lambda e, tt=tt: e.max(m8[:, tt * 8:(tt + 1) * 8], score[:, tt * 32:(tt + 1) * 32]),
                                  reads=("score",), writes=("m8",))
                        for tt in range(4):
                            sc.op("dve", lambda e, tt=tt: e.tensor_scalar(impt[:, tt * 32:(tt + 1) * 32], score[:, tt * 32:(tt + 1) * 32],
                                                                          m8[:, tt * 8 + 7:tt * 8 + 8], 1.0, ALU.is_ge, ALU.subtract),
                                  reads=("score", "m8"), writes=("impt",))
                        sg_ = selt[:, g * 128:(g + 1) * 128]
                        sc.op("dve", lambda e: e.tensor_scalar(sg_, impt[:, :], -NEGM, None, ALU.mult),
                              reads=("impt",), writes=(("selt", g),))

                        def sel_pe(g=g, sg_=sg_):
                            sc.op("pe", lambda e: e.transpose(PS7b[:, 0:128], sg_, ident),
                                  reads=(("selt", g), "cb"), writes=("psI",))
                            for tt in range(4):
                                sc.op("dve", lambda e, tt=tt: e.tensor_copy(Qh[4 * g][64:96, c * 512 + tt * 128:c * 512 + (tt + 1) * 128],
                                                                            PS7b[tt * 32:(tt + 1) * 32, 0:128]),
                                      reads=("psI",), writes=(("Qs", 4 * g, c),))
                            for k_ in range(1, 4):
                                sc.op("pool", lambda e, k_=k_: e.tensor_copy(Qh[4 * g + k_][64:96, cols], Qh[4 * g][64:96, cols]),
                                      reads=(("Qs", 4 * g, c),), writes=(("Qs", 4 * g + k_, c),))
                        deferred.append((u["idx"] + 6, sel_pe))
                    return
                which = 0 if u["kind"] == "s" else 1
                for sb_ in u["subs"]:
                    kt, qlo, qhi, off = sb_["kt"], sb_["qlo"], sb_["qhi"], sb_["off"]
                    n_ = qhi - qlo
                    vb = ((which * 16 + kt) * 2 + g) * 128
                    sc.op("pe", lambda e: e.matmul(ORS[:, qlo:qhi], vtok[:, vb:vb + 128], PT[r][:, off:off + n_],
                                                   start=sb_["sfirst"], stop=sb_["slast"], skip_group_check=True),
                          reads=(("PT", r), ("vtok", which, kt // 4)), writes=(("ors", o, 0), ("ors", o, 1)))

            emit_S(units[0], 0)
            emit_S(units[1], 1)
            pending = []
            deferred = []
            for i_, u in enumerate(units):
                u["idx"] = i_
                r = i_ % 3
                if i_ + 2 < len(units):
                    emit_S(units[i_ + 2], (i_ + 2) % 3)
                emit_exp(u, r)
                keep = []
                for ep in pending:
                    if ep[0] <= i_:
                        epilogue(*ep[1])
                    else:
                        keep.append(ep)
                pending = keep
                for d_ in [d_ for d_ in deferred if d_[0] <= i_]:
                    d_[1]()
                    deferred.remove(d_)
                emit_PV(u, r)
                act_ = next(filler, "done")
                if act_ == "done" or c < 2:
                    if NDUM and c >= 2:
                        sc.op("pe", lambda e: e.matmul(PS[7][:, 0:NDUM], ident, mc[:, 0:NDUM], start=True, stop=True),
                              reads=("cb", "mc"), writes=(("ors", 2, 0), ("ors", 2, 1)))
                elif act_ is not None:
                    act_()
                if u["last"]:
                    j_ = i_
                    while not units[j_]["first"]:
                        j_ -= 1
                    nxt_long = (i_ + 1 < len(units)) and not (units[i_ + 1]["kind"] == "c")
                    nxt_long = (i_ + 1 < len(units)) and not (units[i_ + 1]["kind"] == "c")
                    pending.append((i_ + (2 if nxt_long else 1), (u["h"], c, u["br"], units[j_]["o"], u["br"] == 2)))
            for ep in pending:
                epilogue(*ep[1])
            for d_ in deferred:
                d_[1]()
            for act_ in filler:
                if act_:
                    act_()

        if stop_after == "attn":
            dbg["cat"] = (catT[:, :], ("cat", 7, 3), [128, 8 * S], BF16)
            return finish(nc, sc, es, yT, dbg, catT, debug)

        xre = [A1[:, i_ * 512:(i_ + 1) * 512] for i_ in range(8)]
        ost = [A1[:, 4096 + i_ * 512:4096 + (i_ + 1) * 512] for i_ in range(8)]
        sc.fence([("xre", i_) for i_ in range(8)] + [("ost", i_) for i_ in range(8)])
        for act_ in outproj_slots(3, [0, 1, 2], 0, nbuf=8):
            if act_:
                act_()
        sc.wait_all_dma("sp", ["ost%d" % i_ for i_ in range(8)])
    return nc


def finish(nc, sc, es, yT, dbg, catT, debug):
    for name, (ap, key, shape, dt) in dbg.items():
        d = nc.dram_tensor("dbg_" + name, list(shape), dt, kind="ExternalOutput").ap()
        sc.dma("sp", d, ap, (key,), (), "dbg_" + name)
    sc.wait_all_dma("sp", [k for k in sc.dsems if k.startswith("dbg_")])
    return nc


def prep_shared(inp):
    f = np.float32
    m = {}
    m["w_in"] = np.ascontiguousarray(inp["w_in"][0], dtype=f)
    m["w_out"] = np.ascontiguousarray(inp["w_out"][0], dtype=f)
    vec = np.zeros((128, V_W), f)
    vec[:, V_G:V_G + 8] = inp["norm_g"][0].reshape(8, 128).T
    cw = inp["conv_w"][0]
    vec[:, V_CW:V_CW + 16] = cw.reshape(4, 4, 128).transpose(2, 1, 0).reshape(128, 16)
    for col, name in ((V_CB, "conv_b"), (V_BA, "lru_ba"), (V_BI, "lru_bi"), (V_LAM, "lru_lambda")):
        vec[:, col:col + 4] = inp[name][0].reshape(4, 128).T
    for col, name in ((V_GQ, "q_norm_g"), (V_GKS, "ks_norm_g"), (V_GKW, "kw_norm_g"), (V_GKC, "kc_norm_g")):
        vec[:, col] = np.tile(inp[name][0], 2)
    m["vec"] = vec
    for name, key in (("wa", "lru_wa"), ("wi", "lru_wi")):
        w = inp[key][0]
        bd = np.zeros((128, 4, 128), f)
        for n_ in range(8):
            bd[(n_ % 2) * 64:(n_ % 2) * 64 + 64, n_ // 2, (n_ % 2) * 64:(n_ % 2) * 64 + 64] = w[n_]
        m[name] = np.ascontiguousarray(bd.reshape(128, 512))
    pos = inp["cmp_pos"][0]
    m["posT2"] = np.ascontiguousarray(pos.reshape(16, 2, 64).transpose(1, 2, 0).reshape(128, 16), dtype=f)
    for name, key in (("w1k", "cmp_k_w1"), ("w1v", "cmp_v_w1")):
        w = inp[key][0]
        m[name] = np.ascontiguousarray(w.reshape(16, 2, 64, 256).transpose(1, 2, 0, 3).reshape(128, 4096), dtype=f)
    for name, key in (("w2k", "cmp_k_w2"), ("w2v", "cmp_v_w2")):
        w = inp[key][0]
        m[name] = np.ascontiguousarray(w.reshape(2, 128, 64).transpose(1, 0, 2).reshape(128, 128), dtype=f)
    m.update(host_consts())
    return m


def kernel(**inputs):
    inp = {k: np.asarray(v) for k, v in inputs.items()}
    shared = prep_shared(inp)
    nc = build()
    in_maps = []
    for b in range(8):
        mm = dict(shared)
        mm["xT"] = np.ascontiguousarray(inp["x"][b].T, dtype=np.float32)
        in_maps.append(mm)
    res = run_bass_kernel_spmd(nc, in_maps, core_ids=list(range(8)))
    out = np.stack([np.ascontiguousarray(r["yT"].T) for r in res.results], axis=0)
    return out.astype(np.float32)
```
